# Optimizing a Trainium2 kernel written in Bass

```python
import jax, jax.numpy as jnp
from jax import lax
import numpy as np

D_MODEL = 2048
BATCH = 2
SEQ = 8192
DEPTH = 2

HEAD_DIM = 64
A_HEADS = 8
A_WIDTH = A_HEADS * HEAD_DIM
DILATED_PAIRS = ((128, 1), (512, 4), (2048, 16))
B_WIDTH = 512
CONV_WIDTH = 31
C_Q_HEADS = 16
C_KV_HEADS = 2
C_WIDTH = C_Q_HEADS * HEAD_DIM
C_WINDOW = 128
MIX_WIDTH = A_WIDTH + B_WIDTH + C_WIDTH
IN_WIDTHS = (A_WIDTH, A_WIDTH, A_WIDTH,
             B_WIDTH, B_WIDTH,
             C_WIDTH, C_KV_HEADS * HEAD_DIM, C_KV_HEADS * HEAD_DIM)
IN_WIDTH = sum(IN_WIDTHS)
D_FF = -(-(8 * D_MODEL) // (3 * 256)) * 256
BLOCK = 128
EPS = 1e-6

kernel_name = "hybrid_dilated_conformer_swa_sink_block"


def _rms_norm(x, g):
    xf = x.astype(jnp.float32)
    y = xf * lax.rsqrt(jnp.mean(xf * xf, axis=-1, keepdims=True) + EPS)
    return (y * g.astype(jnp.float32)).astype(x.dtype)


def _layer_norm(x, g, b):
    xf = x.astype(jnp.float32)
    mu = jnp.mean(xf, axis=-1, keepdims=True)
    var = jnp.mean(jnp.square(xf - mu), axis=-1, keepdims=True)
    y = (xf - mu) * lax.rsqrt(var + EPS)
    return (y * g.astype(jnp.float32) + b.astype(jnp.float32)).astype(x.dtype)


def _band_attention(q, k, v, max_dist, sinks=None):
    N, L, Hq, hd = q.shape
    Hk = k.shape[2]
    G = Hq // Hk
    blk = min(BLOCK, L)
    nb = -(-L // blk)
    Lp = nb * blk
    C = blk + max_dist
    qb = jnp.pad(q, ((0, 0), (0, Lp - L), (0, 0), (0, 0))).reshape(N, nb, blk, Hk, G, hd)
    kp = jnp.pad(k, ((0, 0), (max_dist, Lp - L), (0, 0), (0, 0)))
    vp = jnp.pad(v, ((0, 0), (max_dist, Lp - L), (0, 0), (0, 0)))
    idx = jnp.arange(nb)[:, None] * blk + jnp.arange(C)[None, :]
    kb = kp[:, idx]
    vb = vp[:, idx]
    s = jnp.einsum('nbqhgd,nbkhd->nbhgqk', qb, kb,
                   preferred_element_type=jnp.float32) * (hd ** -0.5)
    qpos = jnp.arange(nb)[:, None] * blk + jnp.arange(blk)[None, :]
    kpos = idx - max_dist
    dist = qpos[:, :, None] - kpos[:, None, :]
    valid = (dist >= 0) & (dist <= max_dist) & (kpos[:, None, :] >= 0)
    s = jnp.where(valid[None, :, None, None], s, -jnp.inf)
    m = jnp.max(s, axis=-1)
    if sinks is not None:
        sk = sinks.astype(jnp.float32).reshape(Hk, G)[None, None, :, :, None]
        m = jnp.maximum(m, sk)
    p = jnp.exp(s - m[..., None])
    denom = jnp.sum(p, axis=-1)
    if sinks is not None:
        denom = denom + jnp.exp(sk - m)
    o = jnp.einsum('nbhgqk,nbkhd->nbqhgd', p, vb.astype(jnp.float32))
    denom_t = jnp.transpose(denom, (0, 1, 4, 2, 3))
    o = o / denom_t[..., None]
    lse = jnp.transpose(m, (0, 1, 4, 2, 3)) + jnp.log(denom_t)
    o = o.reshape(N, Lp, Hq, hd)[:, :L]
    lse = lse.reshape(N, Lp, Hq)[:, :L]
    return o, lse


def _dilated_mixture(q, k, v):
    B, S, H, hd = q.shape
    outs, lses = [], []
    for (w, d) in DILATED_PAIRS:
        def to_res(t):
            return t.reshape(B, S // d, d, H, hd).transpose(0, 2, 1, 3, 4).reshape(B * d, S // d, H, hd)
        o, lse = _band_attention(to_res(q), to_res(k), to_res(v), w // d)
        outs.append(o.reshape(B, d, S // d, H, hd).transpose(0, 2, 1, 3, 4).reshape(B, S, H, hd))
        lses.append(lse.reshape(B, d, S // d, H).transpose(0, 2, 1, 3).reshape(B, S, H))
    wts = jax.nn.softmax(jnp.stack(lses, axis=0), axis=0)
    return jnp.einsum('cbsh,cbshd->bshd', wts, jnp.stack(outs, axis=0))


def _conformer_conv(u, gate, conv_w, conv_b, ln_g, ln_b):
    h = u * jax.nn.sigmoid(gate)
    C = h.shape[-1]
    y = lax.conv_general_dilated(h, conv_w[:, None, :].astype(h.dtype), window_strides=(1,),
                                 padding=[(CONV_WIDTH - 1, 0)],
                                 dimension_numbers=('NWC', 'WIO', 'NWC'),
                                 feature_group_count=C)
    y = y + conv_b
    return jax.nn.silu(_layer_norm(y, ln_g, ln_b))


def setup_inputs(seed: int = 0) -> dict:
    key = jax.random.key(seed)
    ks = jax.random.split(key, 17)
    f32 = jnp.float32
    nrm = lambda k, shp: jax.random.normal(k, shp, dtype=f32)
    return {
        "x": nrm(ks[0], (BATCH, SEQ, D_MODEL)),
        "norm1_g": 1.0 + 0.02 * nrm(ks[1], (DEPTH, D_MODEL)),
        "w_in": nrm(ks[2], (DEPTH, D_MODEL, IN_WIDTH)) * D_MODEL ** -0.5,
        "a_q_g": 1.0 + 0.02 * nrm(ks[3], (DEPTH, HEAD_DIM)),
        "a_k_g": 1.0 + 0.02 * nrm(ks[4], (DEPTH, HEAD_DIM)),
        "conv_w": nrm(ks[5], (DEPTH, CONV_WIDTH, B_WIDTH)) * CONV_WIDTH ** -0.5,
        "conv_b": 0.02 * nrm(ks[6], (DEPTH, B_WIDTH)),
        "conv_ln_g": 1.0 + 0.02 * nrm(ks[7], (DEPTH, B_WIDTH)),
        "conv_ln_b": 0.02 * nrm(ks[8], (DEPTH, B_WIDTH)),
        "c_q_g": 1.0 + 0.02 * nrm(ks[9], (DEPTH, HEAD_DIM)),
        "c_k_g": 1.0 + 0.02 * nrm(ks[10], (DEPTH, HEAD_DIM)),
        "c_sinks": 0.5 * nrm(ks[11], (DEPTH, C_Q_HEADS)),
        "w_out": nrm(ks[12], (DEPTH, MIX_WIDTH, D_MODEL)) * MIX_WIDTH ** -0.5,
        "norm2_g": 1.0 + 0.02 * nrm(ks[13], (DEPTH, D_MODEL)),
        "w_gate": nrm(ks[14], (DEPTH, D_MODEL, D_FF)) * D_MODEL ** -0.5,
        "w_up": nrm(ks[15], (DEPTH, D_MODEL, D_FF)) * D_MODEL ** -0.5,
        "w_down": nrm(ks[16], (DEPTH, D_FF, D_MODEL)) * D_FF ** -0.5,
    }


def reference(x, norm1_g, w_in, a_q_g, a_k_g, conv_w, conv_b, conv_ln_g, conv_ln_b,
              c_q_g, c_k_g, c_sinks, w_out, norm2_g, w_gate, w_up, w_down):
    B, S, _ = x.shape
    splits = [int(s) for s in np.cumsum(IN_WIDTHS)[:-1]]
    for l in range(DEPTH):
        h = _rms_norm(x, norm1_g[l])
        proj = h @ w_in[l]
        aq, ak, av, bu, bg, cq, ck, cv = jnp.split(proj, splits, axis=-1)
        aq = _rms_norm(aq.reshape(B, S, A_HEADS, HEAD_DIM), a_q_g[l])
        ak = _rms_norm(ak.reshape(B, S, A_HEADS, HEAD_DIM), a_k_g[l])
        av = av.reshape(B, S, A_HEADS, HEAD_DIM)
        out_a = _dilated_mixture(aq, ak, av).reshape(B, S, A_WIDTH)
        out_b = _conformer_conv(bu, bg, conv_w[l], conv_b[l], conv_ln_g[l], conv_ln_b[l])
        cq = _rms_norm(cq.reshape(B, S, C_Q_HEADS, HEAD_DIM), c_q_g[l])
        ck = _rms_norm(ck.reshape(B, S, C_KV_HEADS, HEAD_DIM), c_k_g[l])
        cv = cv.reshape(B, S, C_KV_HEADS, HEAD_DIM)
        out_c, _ = _band_attention(cq, ck, cv, C_WINDOW - 1, sinks=c_sinks[l])
        out_c = out_c.reshape(B, S, C_WIDTH)
        mix = jnp.concatenate([out_a.astype(x.dtype), out_b.astype(x.dtype),
                               out_c.astype(x.dtype)], axis=-1)
        x = x + mix @ w_out[l]
        h2 = _rms_norm(x, norm2_g[l])
        x = x + (jax.nn.silu(h2 @ w_gate[l]) * (h2 @ w_up[l])) @ w_down[l]
    return x
```

```python
from contextlib import ExitStack
import numpy as np
import concourse.bass as bass
import concourse.mybir as mybir
from concourse.bass_utils import run_bass_kernel_spmd

F32 = mybir.dt.float32
BF16 = mybir.dt.bfloat16
AF = mybir.ActivationFunctionType
ALU = mybir.AluOpType
AX = mybir.AxisListType

D = 2048
T_OWN = 2048
NG = 4
GT = 512
DFF = 5632
EPS = 1e-6
NCORES = 8
CC_ROWS = 128

C_AQ, C_AK, C_AV, C_BU, C_BG, C_CQ, C_CK = 0, 512, 1024, 1536, 2048, 2560, 3584

O_G1, O_G2 = 0, 16
O_CW = 32
O_CB = O_CW + 124
O_LG = O_CB + 4
O_LB = O_LG + 4
O_VALID = O_LB + 4
O_AQG = O_VALID + 1
O_AKG = O_AQG + 64
O_CQG = O_AKG + 64
O_CKG = O_CQG + 64
O_SINK = O_CKG + 64
O_VALID2 = O_SINK + 16
NCST = O_VALID2 + 1


class _Op:
    __slots__ = ("eng", "fn", "deps", "sig", "tick", "dma", "semval")


class FW:
    ENGS = ("pe", "act", "dve", "pool", "sp")

    def __init__(self, nc, same_engine_sync=True):
        self.nc = nc
        self.ops = {e: [] for e in self.ENGS}
        self.lastw = {}
        self.readers = {}
        self.dma_count = {}
        self.dma_last = {}
        self.ses = same_engine_sync
        self.dry = False

    def op(self, eng, fn, reads=(), writes=(), dma=None):
        if self.dry:
            return None
        o = _Op()
        o.eng, o.fn, o.deps, o.sig, o.tick, o.dma, o.semval = eng, fn, [], False, 0, dma, 0
        deps = []
        for r in reads:
            w = self.lastw.get(r)
            if w is not None:
                deps.append(w)
        for w_ in writes:
            w = self.lastw.get(w_)
            if w is not None:
                deps.append(w)
            deps.extend(self.readers.get(w_, ()))
        if dma is not None:
            prev = self.dma_last.get(dma)
            if prev is not None:
                deps.append(prev)
            self.dma_count[dma] = self.dma_count.get(dma, 0) + 1
            o.semval = 16 * self.dma_count[dma]
            self.dma_last[dma] = o
        seen = set()
        for d in deps:
            if d is o or id(d) in seen:
                continue
            seen.add(id(d))
            if d.dma is None and d.eng == eng:
                if eng == "pe":
                    continue
                if (not self.ses) and dma is None:
                    continue
            if d.dma is None:
                d.sig = True
            o.deps.append(d)
        for r in reads:
            self.readers.setdefault(r, []).append(o)
        for w_ in writes:
            self.lastw[w_] = o
            self.readers[w_] = []
        self.ops[eng].append(o)
        return o

    def emit(self):
        nc = self.nc
        for e in self.ENGS:
            t = 0
            for o in self.ops[e]:
                if o.dma is None and o.sig:
                    t += 1
                    o.tick = t
        with ExitStack() as st:
            esem = {e: st.enter_context(nc.semaphore("s_" + e)) for e in self.ENGS}
            dsem = {k: st.enter_context(nc.semaphore("d_%d" % i))
                    for i, k in enumerate(self.dma_count)}
            block = st.enter_context(nc.Block())
            ops = self.ops

            def run(e, eng):
                waited = {}
                for o in ops[e]:
                    need = {}
                    for d in o.deps:
                        if d.dma is not None:
                            key, s, v = ("d", d.dma), dsem[d.dma], d.semval
                        else:
                            key, s, v = ("e", d.eng), esem[d.eng], d.tick
                        if v > need.get(key, (None, 0))[1]:
                            need[key] = (s, v)
                    for key, (s, v) in need.items():
                        if waited.get(key, 0) >= v:
                            continue
                        eng.wait_ge(s, v)
                        waited[key] = v
                    ins = o.fn(eng)
                    if ins is None:
                        continue
                    if o.dma is not None:
                        ins.then_inc(dsem[o.dma], 16)
                    elif o.sig:
                        ins.then_inc(esem[e], 1)

            @block.tensor
            def _(eng):
                run("pe", eng)

            @block.scalar
            def _(eng):
                run("act", eng)

            @block.vector
            def _(eng):
                run("dve", eng)

            @block.gpsimd
            def _(eng):
                run("pool", eng)

            @block.sync
            def _(eng):
                run("sp", eng)


FFN_SEGS = [[0, 1, 2], [3, 4, 5], [6, 7, 8], [9, 10]]


def build_program(debug=False, cfg_over=None, same_engine_sync=True, nlayers=2):
    nc = bass.Bass("TRN2", target_bir_lowering=False)
    dt_in = lambda name, shape: nc.dram_tensor(name, shape, F32, kind="ExternalInput").ap()
    x_own = dt_in("x_own", [T_OWN, D])
    x_halo = dt_in("x_halo", [T_OWN, D])
    x_h2 = dt_in("x_h2", [T_OWN, D])
    w_in_all = dt_in("w_in", [2, D, 3840])
    w_out_all = dt_in("w_out", [2, D, D])
    w_gate_all = dt_in("w_gate", [2, D, DFF])
    w_up_all = dt_in("w_up", [2, D, DFF])
    w_down_all = dt_in("w_down", [2, DFF, D])
    cst_all = dt_in("cst", [2, 128, NCST])
    y1 = nc.dram_tensor("y1", [T_OWN, D], F32).ap()
    y1h = nc.dram_tensor("y1h", [T_OWN, D], F32).ap()
    wc = nc.dram_tensor("wc", [104, 128, 8192], BF16).ap()
    wc_slot = {}
    L = {"l": 0}
    ident_d = dt_in("ident", [128, 128])
    maskA_d = dt_in("maskA", [128, 17 * 128])
    maskC_d = dt_in("maskC", [128, 2 * 128])
    y = nc.dram_tensor("y", [T_OWN, D], F32, kind="ExternalOutput").ap()
    kt_d = nc.dram_tensor("kt_d", [4, 128, 12 * 512], BF16).ap()
    v_d = nc.dram_tensor("v_d", [4, 128, 48, 130], BF16).ap()
    dbg = None
    if debug:
        dbg = nc.dram_tensor("dbg", [NG, 128, 16, GT], BF16, kind="ExternalOutput").ap()

    cfg = dict(ipstop=99, halo=4, own=NG, attnA=True, conv=True, attnC=True, outp=True, ffn=True, inproj=True, kvdram=True)
    cfg.update(cfg_over or {})
    fw = FW(nc, same_engine_sync=same_engine_sync)
    st = ExitStack()
    sb = lambda name, shape, dt: st.enter_context(nc.sbuf_tensor(name, shape, dt))
    ps = lambda name, shape, dt: st.enter_context(nc.psum_tensor(name, shape, dt))

    xg = sb("xg", [128, 4, D], F32)
    xn = sb("xn", [128, D], BF16)
    hT = sb("hT", [128, 16, GT], BF16)
    Wb = [sb("W%d" % i, [128, 16, 512], BF16) for i in range(3)]
    sq2 = [sb("sq0", [128, 512], F32), sb("sq1", [128, 512], F32)]
    qn2 = [sb("qn%d" % i, [128, 512], BF16) for i in range(3)]
    QTA = sb("QTA", [128, 2, 4, GT], BF16)
    QTC = sb("QTC", [128, 8, GT], BF16)
    KTs = sb("KTs", [128, 4, GT], BF16)
    Vs = sb("Vs", [128, 4, 8, 65], BF16)
    KTb = sb("KTb", [128, 2560], BF16)
    Vb = sb("Vb", [128, 20, 130], BF16)
    ckd2 = [sb("ckd%d" % i, [128, 2, 2, 64], BF16) for i in range(2)]
    KTC = sb("KTC", [128, 2, 640], BF16)
    VC = sb("VC", [128, 5, 2, 65], BF16)
    hg = sb("hg", [128, 512], F32)
    hgT = sb("hgT", [128, 4, 544], F32)
    cbuf = sb("cbuf", [128, 3072], F32)
    yT = cbuf[:, 0:2048].rearrange("p (c t) -> p c t", t=512)
    lnA = cbuf[:, 2048:2560]
    lnB = cbuf[:, 2560:3072]
    lnC = sb("lnC", [128, 512], F32)
    Eb = [sb("E%d" % i, [128, 512], BF16) for i in range(4)]
    Pb = [sb("P%d" % i, [128, 512], BF16) for i in range(4)]
    mixtok = sb("mixtok", [128, 4, 128], BF16)
    mixT = sb("mixT", [128, 16, GT], BF16)
    actT = cbuf[:, :].bitcast(BF16).rearrange("p (k t) -> p k t", t=GT)
    sg = sb("sg", [128, 512], F32)
    maskA = sb("maskA_s", [128, 17 * 128], BF16)
    maskC = sb("maskC_s", [128, 2 * 128], BF16)
    ident_f = sb("ident_f", [128, 128], F32)
    ident_b = sb("ident_b", [128, 128], BF16)
    cst = sb("cst_s", [128, NCST], F32)
    zeros_b = sb("zeros_b", [128, 512], BF16)
    onesm = sb("onesm", [128, 128], F32)
    small = sb("small", [128, 128], F32)
    es = sb("es", [128, 16], F32)
    epsb = sb("epsb", [128, 1], F32)

    pmm = [ps("pmm%d" % i, [128, 512], F32) for i in range(4)]
    ptr = [ps("ptr%d" % i, [128, 1024], BF16) for i in range(2)]
    pf = [ps("pf%d" % i, [128, 512], F32) for i in range(2)]

    S_SS, S_MS, S_SD, S_RS = 0, 4, 8, 12
    S_H = 16
    S_DEN, S_REC = 48, 56
    S_H2 = 64

    def act_op(out, in_, func, reads, writes, **kw):
        return fw.op("act", lambda e, o=out, i=in_, f=func, kw=kw: e.activation(o, i, f, **kw), reads, writes)

    def tt(out, in0, in1, op, reads, writes, eng="dve"):
        return fw.op(eng, lambda e, o=out, a=in0, b=in1, p=op: e.tensor_tensor(o, a, b, p), reads, writes)

    def tsc(out, in0, s1, s2, op0, op1, reads, writes, eng="dve"):
        if op1 is None:
            return fw.op(eng, lambda e, o=out, a=in0, s1=s1, p0=op0: e.tensor_scalar(o, a, s1, None, p0), reads, writes)
        return fw.op(eng, lambda e, o=out, a=in0, s1=s1, s2=s2, p0=op0, p1=op1:
                     e.tensor_scalar(o, a, s1, s2, p0, p1), reads, writes)

    def copy(eng, out, in_, reads, writes):
        if eng == "act":
            return fw.op("act", lambda e, o=out, i=in_: e.copy(o, i), reads, writes)
        return fw.op(eng, lambda e, o=out, i=in_: e.tensor_copy(o, i), reads, writes)

    def recip(out, in_, reads, writes):
        return fw.op("dve", lambda e, o=out, i=in_: e.reciprocal(o, i), reads, writes)

    def mm(out, lhsT, rhs, start, stop, reads, writes):
        return fw.op("pe", lambda e, o=out, l=lhsT, r=rhs, s=start, t=stop:
                     e.matmul(o, l, r, start=s, stop=t), reads, writes)

    def trn(out, in_, identity, reads, writes):
        return fw.op("pe", lambda e, o=out, i=in_, d=identity: e.transpose(o, i, d), reads, writes)

    def dma(q, out, in_, reads, writes, key):
        return fw.op(q, lambda e, o=out, i=in_: e.dma_start(out=o, in_=i), reads, writes, dma=key)

    def rstd_from_ss(ncol, c_ss, c_ms, c_sd, c_rs, inv_n, tag):
        r = ("small", tag)
        tsc(small[:, c_ms:c_ms + ncol], small[:, c_ss:c_ss + ncol], inv_n, EPS, ALU.mult, ALU.add, [r], [r])
        act_op(small[:, c_sd:c_sd + ncol], small[:, c_ms:c_ms + ncol], AF.Sqrt, [r], [r])
        recip(small[:, c_rs:c_rs + ncol], small[:, c_sd:c_sd + ncol], [r], [r])

    plan = []
    wstate = {"loaded": 0, "next": 0, "released": set()}

    def _wsrc(req):
        kind = req[0]
        l = req[-1]
        req = req[:-1]
        w_in, w_out, w_gate, w_up, w_down = (w_in_all[l], w_out_all[l], w_gate_all[l], w_up_all[l], w_down_all[l])
        if kind == "in":
            _, c0, n = req
            return w_in[:, c0:c0 + n].rearrange("(k p) c -> p k c", p=128), 16, n
        if kind == "out":
            n = req[1]
            return w_out[:, n * 512:(n + 1) * 512].rearrange("(k p) c -> p k c", p=128), 16, 512
        if kind == "gate":
            b = req[1]
            return w_gate[:, b * 512:(b + 1) * 512].rearrange("(k p) c -> p k c", p=128), 16, 512
        if kind == "up":
            b = req[1]
            return w_up[:, b * 512:(b + 1) * 512].rearrange("(k p) c -> p k c", p=128), 16, 512
        if kind == "down":
            _, c0, nch, n = req
            return (w_down[c0 * 128:(c0 + nch) * 128, n * 512:(n + 1) * 512]
                    .rearrange("(k p) c -> p k c", p=128), nch, 512)
        raise ValueError(kind)

    def _wpump():
        while wstate["loaded"] < len(plan):
            i = wstate["loaded"]
            if i >= 3 and (i - 3) not in wstate["released"]:
                break
            src, nk, n = _wsrc(plan[i])
            s = i % 3
            key = plan[i]
            if key in wc_slot:
                sl = wc_slot[key]
                dma("pool", Wb[s][:, 0:nk, 0:n], wc[sl, :, 0:nk * n].rearrange("p (k c) -> p k c", c=n),
                    [("wc", sl)], [("W", s)], ("W", s))
            else:
                sl = len(wc_slot)
                wc_slot[key] = sl
                dma("pool", Wb[s][:, 0:nk, 0:n], src, [], [("W", s)], ("W", s))
                dma("sp", wc[sl, :, 0:nk * n].rearrange("p (k c) -> p k c", c=n), Wb[s][:, 0:nk, 0:n],
                    [("W", s)], [("wc", sl)], ("wcst", s))
            wstate["loaded"] += 1

    def wget(*req):
        req = req + (L["l"],)
        i = wstate["next"]
        wstate["next"] += 1
        if fw.dry:
            plan.append(req)
            return i, i % 3
        assert plan[i] == req, (i, plan[i], req)
        _wpump()
        assert wstate["loaded"] > i
        return i, i % 3

    def wrel(i):
        if fw.dry:
            return
        wstate["released"].add(i)
        _wpump()

    def prologue_layer(l):
        dma("sp", cst[:, :], cst_all[l], [], ["cst"], "cst")
        act_op(es[:, :], cst[:, O_SINK:O_SINK + 16], AF.Exp, ["cst"], ["es"])
        tt(cst[:, O_AKG:O_AKG + 64], cst[:, O_AKG:O_AKG + 64], cst[:, O_AQG:O_AQG + 64], ALU.mult, ["cst"], ["cst"])
        tt(cst[:, O_CKG:O_CKG + 64], cst[:, O_CKG:O_CKG + 64], cst[:, O_CQG:O_CQG + 64], ALU.mult, ["cst"], ["cst"])

    def prologue():
        dma("sp", ident_f[:, :], ident_d, [], ["ident_f"], "identf")
        dma("pool", ident_b[:, :], ident_d, [], ["ident_b"], "identb")
        dma("pool", maskA[:, 0:1024], maskA_d[:, 0:1024], [], ["maskA"], "maskA")
        dma("pool", maskA[:, 1024:2176], maskA_d[:, 1024:2176], [], ["maskA"], "maskA")
        dma("pool", maskC[:, :], maskC_d, [], ["maskC"], "maskC")
        fw.op("dve", lambda e: e.memset(zeros_b[:, :], 0.0), [], ["zeros"])
        fw.op("dve", lambda e: e.memset(onesm[:, :], 1.0 / 512.0), [], ["onesm"])
        fw.op("dve", lambda e: e.memset(epsb[:, :], EPS), [], ["epsb"])
        fw.op("dve", lambda e: e.memset(QTA[:, :, :, :], 0.0), [], ["QTA"])

    def load_and_norm(src_rows, gcol, load):
        for t in range(4):
            if load:
                src_rows(t)
            act_op(xn[:, :], xg[:, t, :], AF.Square, [("xg", t)], [("xn", 0), ("xn", 1), ("small", "n")],
                   accum_out=small[:, S_SS + t:S_SS + t + 1])
        rstd_from_ss(4, S_SS, S_MS, S_SD, S_RS, 1.0 / D, "n")
        for t in range(4):
            for half in range(2):
                act_op(xn[:, half * 1024:(half + 1) * 1024], xg[:, t, half * 1024:(half + 1) * 1024], AF.Copy,
                       [("xg", t), ("small", "n")], [("xn", half)], scale=small[:, S_RS + t:S_RS + t + 1])
            for half in range(2):
                pt = ptr[half]
                for j in range(8):
                    k = half * 8 + j
                    trn(pt[:, j * 128:(j + 1) * 128], xn[:, k * 128:(k + 1) * 128], ident_b[:, :],
                        [("xn", half), "ident_b"], [("ptr", half)])
                for j in range(8):
                    k = half * 8 + j
                    dst = hT[:, k, t * 128:(t + 1) * 128]
                    src = pt[:, j * 128:(j + 1) * 128]
                    gap = cst[:, gcol + k:gcol + k + 1]
                    tsc(dst, src, gap, None, ALU.mult, None, [("ptr", half), "cst"], [("hT", k)])

    bank = {"i": 0}
    deferred = []

    def defer(fn):
        deferred.append(fn)

    def flush_deferred(keep=0):
        while len(deferred) > keep:
            deferred.pop(0)()

    def proj_tile(s, t, ncols, nk=16, lhs=None, lhs_key="hT"):
        keyed = lhs_key in ("hT", "actT")
        lhs = hT if lhs is None else lhs
        b = bank["i"] % 4
        bank["i"] += 1
        for k in range(nk):
            mm(pmm[b][:, 0:ncols], lhs[:, k, t * 128:(t + 1) * 128], Wb[s][:, k, 0:ncols],
               k == 0, k == nk - 1, [(lhs_key, k) if keyed else lhs_key, ("W", s)], [("pmm", b)])
        flush_deferred(keep=1)
        return pmm[b], ("pmm", b)

    qkc = {"i": 0}

    def qk_norm(psrc, pkey, nh, gcol):
        bi = qkc["i"] % 2
        bq = qkc["i"] % 3
        qkc["i"] += 1
        sq, qn = sq2[bi], qn2[bq]
        ks, kn, kh = ("sq", bi), ("qn", bq), ("small", "h", bi)
        so = S_H if bi == 0 else S_H2
        w = nh * 64
        act_op(sq[:, 0:w], psrc[:, 0:w], AF.Square, [pkey], [ks])
        fw.op("dve", lambda e, nh=nh, w=w, sq=sq, so=so: e.tensor_reduce(
            small[:, so:so + nh], sq[:, 0:w].rearrange("p (h d) -> p h d", d=64), AX.X, ALU.add),
            [ks], [kh])
        if cfg.get("rsq", False):
            act_op(small[:, so + 24:so + 32], small[:, so:so + 8], AF.Abs_reciprocal_sqrt, [kh, "epsb"], [kh],
                   scale=1.0 / 64.0, bias=epsb[:, 0:1])
        else:
            tsc(small[:, so + 8:so + 16], small[:, so:so + 8], 1.0 / 64.0, EPS, ALU.mult, ALU.add, [kh], [kh])
            act_op(small[:, so + 16:so + 24], small[:, so + 8:so + 16], AF.Sqrt, [kh], [kh])
            recip(small[:, so + 24:so + 32], small[:, so + 16:so + 24], [kh], [kh])
        rs_b = small[:, so + 24:so + 24 + nh].unsqueeze(2).to_broadcast([128, nh, 64])
        p3 = psrc[:, 0:w].rearrange("p (h d) -> p h d", d=64)
        if gcol is None:
            tt(qn[:, 0:w].rearrange("p (h d) -> p h d", d=64), p3, rs_b, ALU.mult, [pkey, kh], [kn])
        else:
            tt(sq[:, 0:w].rearrange("p (h d) -> p h d", d=64), p3, rs_b, ALU.mult, [pkey, kh], [ks])
            g_b = cst[:, gcol:gcol + 64].unsqueeze(1).to_broadcast([128, nh, 64])
            tt(qn[:, 0:w].rearrange("p (h d) -> p h d", d=64), sq[:, 0:w].rearrange("p (h d) -> p h d", d=64),
               g_b, ALU.mult, [ks, "cst"], [kn])
        return None, ks, qn, kn

    def transpose_pairs(src, src_key, npairs, dst3, t, dst_key, half):
        pt = ptr[half]
        for p in range(npairs):
            trn(pt[:, p * 128:(p + 1) * 128], src[:, p * 128:(p + 1) * 128], ident_b[:, :],
                [src_key, "ident_b"], [("ptr", half)])
        copy("act" if half else "dve", dst3[:, 0:npairs, t * 128:(t + 1) * 128],
             pt[:, 0:npairs * 128].rearrange("p (a b) -> p a b", b=128), [("ptr", half)], [dst_key])

    def in_proj_group(G, kvonly, kv_last, hist, vcol):
        is_halo = vcol is not None
        last_halo = kv_last
        own = not kvonly
        vap = cst[:, vcol:vcol + 1] if vcol is not None else None
        ips = cfg.get("ipstop", 99)
        if own:
            i, s = wget("in", C_AQ, 512)
            for t in range(4):
                if ips >= 2:
                    pp, pk = proj_tile(s, t, 512)
                if ips >= 3:
                    _, _, qn, kn = qk_norm(pp, pk, 8, None)
                if ips >= 4:
                    def aq_tr(qn=qn, kn=kn, t=t):
                        pt = ptr[t % 2]
                        for p in range(4):
                            trn(pt[:, p * 128:(p + 1) * 128], qn[:, p * 128:(p + 1) * 128], ident_b[:, :],
                                [kn, "ident_b"], [("ptr", t % 2)])
                        src3 = pt[:, 0:512].rearrange("p (a b) -> p a b", b=128)
                        copy("dve", QTA[0:64, 0, :, t * 128:(t + 1) * 128], src3[0:64], [("ptr", t % 2)], ["QTA"])
                        copy("act", QTA[64:128, 1, :, t * 128:(t + 1) * 128], src3[64:128], [("ptr", t % 2)], ["QTA"])
                    defer(aq_tr)
            wrel(i)
        if ips <= 4:
            flush_deferred()
            return
        i, s = wget("in", C_AK, 512)
        for t in range(4):
            pp, pk = proj_tile(s, t, 512)
            _, _, qn, kn = qk_norm(pp, pk, 8, O_AKG)
            defer(lambda qn=qn, kn=kn, t=t: transpose_pairs(qn, kn, 4, KTs, t, "KTs", t % 2))
        wrel(i)
        flush_deferred()
        if cfg["kvdram"]:
            dma("sp", kt_d[:, :, G * 512:(G + 1) * 512].rearrange("a q t -> q a t"), KTs[:, :, :],
                ["KTs"], [("kt_d", G)], "ktw")
        if ips <= 5:
            return
        i, s = wget("in", C_AV, 512)
        if is_halo:
            tsc(Vs[:, :, :, 64:65], zeros_b[:, 0:32].rearrange("p (a b c) -> p a b c", a=4, b=8),
                1.0, vap, ALU.add, ALU.mult, ["zeros", "cst"], ["Vs"])
        else:
            tsc(Vs[:, :, :, 64:65], zeros_b[:, 0:32].rearrange("p (a b c) -> p a b c", a=4, b=8),
                1.0, None, ALU.add, None, ["zeros"], ["Vs"])
        for t in range(4):
            pp, pk = proj_tile(s, t, 512)
            src = pp[:, 0:512].rearrange("p (h d) -> p h d", d=64)
            if is_halo:
                tsc(Vs[:, t, :, 0:64], src, vap, None, ALU.mult, None,
                    [pk, "cst"], ["Vs"])
            else:
                copy("dve", Vs[:, t, :, 0:64], src, [pk], ["Vs"])
        wrel(i)
        for a in range(4 if cfg["kvdram"] else 0):
            dma("sp", v_d[a, :, 4 * G:4 * G + 4, :],
                Vs[:, :, 2 * a:2 * a + 2, :].rearrange("p t b c -> p t (b c)"), ["Vs"], [("v_d", G, a)], "vw")
        if kvonly and not last_halo:
            flush_deferred()
            return
        if ips <= 6:
            return
        if hist:
            copy("dve", hgT[:, :, 0:32], hgT[:, :, 512:544], ["hgT"], ["hgT"])
        iu, su = wget("in", C_BU, 512)
        ig, sg_ = wget("in", C_BG, 512)
        for t in range(4):
            pu, ku = proj_tile(su, t, 512)
            pg, kg = proj_tile(sg_, t, 512)
            hgb, hk = (hg, "hg") if t % 2 == 0 else (sq2[0], ("sq", 0))
            act_op(hgb[:, :], pg[:, :], AF.Sigmoid, [kg], [hk])
            tt(hgb[:, :], hgb[:, :], pu[:, :], ALU.mult, [hk, ku], [hk])
            if vap is not None:
                tsc(hgb[:, :], hgb[:, :], vap, None, ALU.mult, None, [hk, "cst"], [hk])
            def hg_tr(t=t, hgb=hgb, hk=hk):
                pfb = pf[t % 2]
                for c in range(4):
                    trn(pfb[:, c * 128:(c + 1) * 128], hgb[:, c * 128:(c + 1) * 128], ident_f[:, :],
                        [hk, "ident_f"], [("pf", t % 2)])
                copy("act", hgT[:, :, 32 + t * 128:32 + (t + 1) * 128],
                     pfb[:, :].rearrange("p (c b) -> p c b", b=128), [("pf", t % 2)], ["hgT"])
            defer(hg_tr)
        wrel(iu)
        wrel(ig)
        flush_deferred()
        if ips <= 7:
            return
        if own:
            for half in range(2):
                i, s = wget("in", C_CQ + half * 512, 512)
                for t in range(4):
                    pp, pk = proj_tile(s, t, 512)
                    _, _, qn, kn = qk_norm(pp, pk, 8, None)

                    def cq_tr(qn=qn, kn=kn, t=t, half=half):
                        pt = ptr[t % 2]
                        for p in range(4):
                            trn(pt[:, p * 128:(p + 1) * 128], qn[:, p * 128:(p + 1) * 128], ident_b[:, :],
                                [kn, "ident_b"], [("ptr", t % 2)])
                        copy("act" if t % 2 else "dve", QTC[:, half * 4:half * 4 + 4, t * 128:(t + 1) * 128],
                             pt[:, 0:512].rearrange("p (a b) -> p a b", b=128), [("ptr", t % 2)], ["QTC"])
                    defer(cq_tr)
                wrel(i)
        if ips <= 8:
            return
        if hist:
            copy("dve", KTC[:, :, 0:128], KTC[:, :, 512:640], ["KTC"], ["KTC"])
            copy("dve", VC[:, 0, :, :], VC[:, 4, :, :], ["VC"], ["VC"])
        i, s = wget("in", C_CK, 256)
        vscal = vap
        if is_halo:
            tsc(VC[:, 1:5, :, 64:65], zeros_b[:, 0:8].rearrange("p (a b c) -> p a b c", a=4, b=2),
                1.0, vscal, ALU.add, ALU.mult, ["zeros", "cst"], ["VC"])
        else:
            tsc(VC[:, 1:5, :, 64:65], zeros_b[:, 0:8].rearrange("p (a b c) -> p a b c", a=4, b=2),
                1.0, None, ALU.add, None, ["zeros"], ["VC"])
        cs = cfg.get("cstop", 99)
        for t in range(4):
            if cs < 1:
                continue
            pp, pk = proj_tile(s, t, 256)
            vsrc = pp[:, 128:256].rearrange("p (h d) -> p h d", d=64)
            _, ks_, qn, kn = qk_norm(pp, pk, 4, O_CKG)
            vsrc = pp[:, 128:256].rearrange("p (h d) -> p h d", d=64)
            if is_halo:
                tsc(VC[:, 1 + t, :, 0:64], vsrc, vscal, None, ALU.mult, None, [pk, kn, "cst"], ["VC"])
            else:
                copy("dve", VC[:, 1 + t, :, 0:64], vsrc, [pk, kn], ["VC"])
            if cs < 2:
                continue
            if cs < 3:
                continue
            copy("dve", ckd2[t % 2][:, :, :, :],
                 qn[:, 0:128].rearrange("p (h d) -> p h d", d=64).unsqueeze(2).to_broadcast([128, 2, 2, 64]),
                 [kn], [("ckd", t % 2)])
            def ck_tr(t=t):
                pt = ptr[t % 2]
                for kv in range(2):
                    trn(pt[:, kv * 128:(kv + 1) * 128], ckd2[t % 2][:, kv, :, :].rearrange("p a d -> p (a d)"),
                        ident_b[:, :], [("ckd", t % 2), "ident_b"], [("ptr", t % 2)])
                copy("act" if t % 2 else "dve", KTC[:, :, 128 + t * 128:128 + (t + 1) * 128],
                     pt[:, 0:256].rearrange("p (a b) -> p a b", b=128), [("ptr", t % 2)], ["KTC"])
            defer(ck_tr)
        wrel(i)
        flush_deferred()

    sbanks = [pmm[0], pmm[1], pf[0], pf[1]]
    skeys = [("pmm", 0), ("pmm", 1), ("pf", 0), ("pf", 1)]
    ucnt = {"i": 0}
    LA = 3

    def attention_pair(Kap, Vap, kkey, vkey, Qt, qkey, p, nkt, off, R, mask, mkey, sink_cols, filler):
        for hl in range(2):
            mm(pmm[2 + hl][:, 0:260], zeros_b[:, 0:128], zeros_b[:, 0:260], True, False, ["zeros"], [("pmm", 2 + hl)])
        units = []
        for j in range(nkt):
            i_lo = max(0, j - off)
            i_hi = min(3, j - off + R)
            if i_hi < i_lo:
                continue
            for hl in range(2):
                units.append((j, hl, i_lo, i_hi))
        n = len(units)
        lastu = {hl: max(u for u in range(n) if units[u][1] == hl) for hl in range(2)}
        slots = {}

        def front(u):
            j, hl, i_lo, i_hi = units[u]
            hh = hl * 64
            b = ucnt["i"] % 4
            ucnt["i"] += 1
            slots[u] = b
            nq = i_hi - i_lo + 1
            d_lo = off + i_lo - j
            psb = sbanks[b]
            mm(psb[:, 0:nq * 128], Kap(hh, j), Qt(hl, hh, i_lo * 128, (i_hi + 1) * 128),
               True, True, [kkey, qkey], [skeys[b]])
            act_op(Eb[b][:, 0:nq * 128], psb[:, 0:nq * 128], AF.Exp, [skeys[b]], [("E", b)], scale=0.125)
            tt(Pb[b][:, 0:nq * 128], Eb[b][:, 0:nq * 128], mask[:, d_lo * 128:(d_lo + nq) * 128],
               ALU.mult, [("E", b), mkey], [("P", b)])
            if filler is not None:
                filler()
            if u == 4:
                flush_deferred()

        def back(u):
            j, hl, i_lo, i_hi = units[u]
            b = slots[u]
            po = pmm[2 + hl]
            for i in range(i_lo, i_hi + 1):
                last = (u == lastu[hl]) and (i == i_hi)
                mm(po[:, i * 65:(i + 1) * 65], Pb[b][:, (i - i_lo) * 128:(i - i_lo + 1) * 128], Vap(hl, j),
                   False, last, [("P", b), vkey], [("pmm", 2 + hl)])

        for idx in range(n + LA):
            if idx < n:
                front(idx)
            if idx >= LA:
                back(idx - LA)
        for hl in range(2):
            po = pmm[2 + hl]
            okey = ("pmm", 2 + hl)
            po3 = po[:, 0:260].rearrange("p (i c) -> p i c", c=65)
            den = small[:, S_DEN + 4 * hl:S_DEN + 4 * hl + 4]
            rec = small[:, S_REC + 4 * hl:S_REC + 4 * hl + 4]
            ka = ("small", "a", hl)
            if sink_cols is None:
                tsc(den, po3[:, :, 64:65].rearrange("p i c -> p (i c)"), 1e-30, None, ALU.max, None, [okey], [ka])
            else:
                sc = sink_cols[hl]
                tsc(den, po3[:, :, 64:65].rearrange("p i c -> p (i c)"), es[:, sc:sc + 1], None, ALU.add, None,
                    [okey, "es"], [ka])
            recip(rec, den, [ka], [ka])
            tt(mixtok[:, :, hl * 64:(hl + 1) * 64], po3[:, :, 0:64],
               rec.unsqueeze(2).to_broadcast([128, 4, 64]), ALU.mult, [okey, ka], ["mixtok"])

    def mixtok_to_mixT(chunk, half):
        pt = ptr[half]
        for i in range(4):
            trn(pt[:, i * 128:(i + 1) * 128], mixtok[:, i, :], ident_b[:, :], ["mixtok", "ident_b"], [("ptr", half)])
        copy("act" if half else "dve", mixT[:, chunk, :], pt[:, 0:512], [("ptr", half)], ["mixT"])

    def attn_A(g, filler=None):
        for p in range(4):
            dma("sp", KTb[:, :], kt_d[p, :, 4 * g * 128:(4 * g + 20) * 128],
                [("kt_d", G) for G in range(g, g + 5)], ["KTb"], "ktb")
            dma("sp", Vb[:, :, :], v_d[p, :, 4 * g:4 * g + 20, :],
                [("v_d", G, p) for G in range(g, g + 5)], ["Vb"], "vb")
            attention_pair(lambda hh, j: KTb[:, j * 128:(j + 1) * 128],
                           lambda hl, j: Vb[:, j, hl * 65:(hl + 1) * 65],
                           "KTb", "Vb", (lambda hl, hh, c0, c1, p=p: QTA[:, hl, p, c0:c1]), "QTA",
                           p, 20, 16, 16, maskA, "maskA", None, filler)
            defer(lambda p=p: mixtok_to_mixT(p, p % 2))
        flush_deferred()

    def attn_C(filler=None):
        for p in range(8):
            kv = (2 * p) // 8
            attention_pair(lambda hh, j, kv=kv: KTC[hh:hh + 64, kv, j * 128:(j + 1) * 128],
                           lambda hl, j, kv=kv: VC[:, j, kv, :],
                           "KTC", "VC", (lambda hl, hh, c0, c1, p=p: QTC[hh:hh + 64, p, c0:c1]), "QTC",
                           p, 5, 1, 1, maskC, "maskC", (2 * p, 2 * p + 1), filler)
            defer(lambda p=p: mixtok_to_mixT(8 + p, p % 2))
        flush_deferred()

    def conv_ops():
        ops = []
        wcol = lambda c, j: cst[:, O_CW + c * 31 + j:O_CW + c * 31 + j + 1]
        for c in range(4):
            ops.append(lambda c=c: tsc(yT[:, c, :], hgT[:, c, 2:514], wcol(c, 0), cst[:, O_CB + c:O_CB + c + 1],
                                       ALU.mult, ALU.add, ["hgT", "cst"], [("yT", c)]))
        for j in range(1, 31):
            for c in range(4):
                ops.append(lambda c=c, j=j, wc=wcol(c, j): fw.op("dve", lambda e: e.scalar_tensor_tensor(
                    yT[:, c, :], hgT[:, c, 2 + j:514 + j], wc, yT[:, c, :], ALU.mult, ALU.add),
                    ["hgT", "cst", ("yT", c)], [("yT", c)]))
        return ops

    def ln_part():
        for c in range(4):
            mm(pf[0][:, :], onesm[:, :], yT[:, c, :], c == 0, c == 3, ["onesm", ("yT", c)], [("pf", 0)])
        for c in range(4):
            buf, bk = (lnA, "lnA") if c % 2 == 0 else (lnB, "lnB")
            act_op(buf[:, :], yT[:, c, :], AF.Square, [("yT", c)], [bk])
            mm(pf[1][:, :], onesm[:, :], buf[:, :], c == 0, c == 3, ["onesm", bk], [("pf", 1)])
        copy("act", lnC[:, :], pf[0][:, :], [("pf", 0)], ["lnC"])
        tt(lnA[:, :], lnC[:, :], lnC[:, :], ALU.mult, ["lnC"], ["lnA"])
        tt(lnA[:, :], pf[1][:, :], lnA[:, :], ALU.subtract, [("pf", 1), "lnA"], ["lnA"])
        tsc(lnA[:, :], lnA[:, :], EPS, None, ALU.add, None, ["lnA"], ["lnA"])
        act_op(lnA[:, :], lnA[:, :], AF.Sqrt, ["lnA"], ["lnA"])
        recip(lnA[:, :], lnA[:, :], ["lnA"], ["lnA"])
        for c in range(4):
            tt(lnB[:, :], yT[:, c, :], lnC[:, :], ALU.subtract, [("yT", c), "lnC"], ["lnB"])
            tt(lnB[:, :], lnB[:, :], lnA[:, :], ALU.mult, ["lnB", "lnA"], ["lnB"])
            act_op(mixT[:, 4 + c, :], lnB[:, :], AF.Silu, ["lnB", "cst"], ["mixT"],
                   scale=cst[:, O_LG + c:O_LG + c + 1], bias=cst[:, O_LB + c:O_LB + c + 1])

    def out_proj():
        for n in range(4):
            i, s = wget("out", n)
            for t in range(4):
                pp, pk = proj_tile(s, t, 512, lhs=mixT, lhs_key="mixT")
                tt(xg[:, t, n * 512:(n + 1) * 512], xg[:, t, n * 512:(n + 1) * 512], pp[:, :], ALU.add,
                   [("xg", t), pk], [("xg", t)])
            wrel(i)

    def ffn():
        fw.op("dve", lambda e: e.memset(small[:, 101:102], 0.0), [],
              [("yT", 0), ("yT", 1), ("yT", 2), ("yT", 3), "lnA", "lnB", ("small", "f2")])
        for seg in FFN_SEGS:
            for lb, b in enumerate(seg):
                ig, sgs = wget("gate", b)
                iu, sus = wget("up", b)
                for f in range(4):
                    bg_, bu_ = (0, 1) if f % 2 == 0 else (2, 3)
                    for k in range(16):
                        mm(pmm[bg_][:, :], Wb[sgs][:, k, f * 128:(f + 1) * 128], hT[:, k, :], k == 0, k == 15,
                           [("W", sgs), ("hT", k)], [("pmm", bg_)])
                    for k in range(16):
                        mm(pmm[bu_][:, :], Wb[sus][:, k, f * 128:(f + 1) * 128], hT[:, k, :], k == 0, k == 15,
                           [("W", sus), ("hT", k)], [("pmm", bu_)])
                    act_op(sg[:, :], pmm[bg_][:, :], AF.Silu, [("pmm", bg_)], ["sg"])
                    tt(actT[:, lb * 4 + f, :], sg[:, :], pmm[bu_][:, :], ALU.mult, ["sg", ("pmm", bu_)],
                       [("actT", lb * 4 + f)])
                wrel(ig)
                wrel(iu)
            nch = 4 * len(seg)
            for n in range(4):
                i, s = wget("down", seg[0] * 4, nch, n)
                for t in range(4):
                    pp, pk = proj_tile(s, t, 512, nk=nch, lhs=actT, lhs_key="actT")
                    tt(xg[:, t, n * 512:(n + 1) * 512], xg[:, t, n * 512:(n + 1) * 512], pp[:, :], ALU.add,
                       [("xg", t), pk], [("xg", t)])
                wrel(i)


    def direct_loader(src, g, rkeys):
        def f(t):
            dma("sp", xg[:, t, :], src[g * 512 + t * 128:g * 512 + (t + 1) * 128, :],
                rkeys, [("xg", t)], ("xg", t))
        return f

    def kv_pass(src, srckey, vcol):
        for hg in range(4 - cfg["halo"], 4):
            load_and_norm(direct_loader(src, hg, [(srckey, hg)] if srckey else []), O_G1, True)
            if cfg["inproj"]:
                in_proj_group(hg, True, hg == 3, False, vcol)

    def full_group(src, srckey, g, slot, vcol, dst, dstkey, dbg_g=None):
        load_and_norm(direct_loader(src, g, [(srckey, g)] if srckey else []), O_G1, True)
        if cfg["inproj"]:
            in_proj_group(slot, False, False, True, vcol)
        fw.op("dve", lambda e: e.memset(small[:, 100:101], 0.0), [], [("actT", k) for k in range(12)] + [("small", "f")])
        pend = conv_ops() if cfg["conv"] else []
        fcnt = {"i": 0}

        def filler():
            fcnt["i"] += 1
            if pend and fcnt["i"] % 2 == 0:
                pend.pop(0)()
        if cfg["attnA"]:
            attn_A(slot - 4, filler)
        if cfg["attnC"]:
            attn_C(filler)
        while pend:
            pend.pop(0)()
        if cfg["conv"]:
            ln_part()
        if dbg_g is not None:
            dma("sp", dbg[dbg_g], mixT[:, :, :], ["mixT"], [("dbg", dbg_g)], "dbg")
        if cfg["outp"]:
            out_proj()
        if cfg["ffn"]:
            load_and_norm(None, O_G2, False)
            ffn()
        dma("sp", dst[g * 512:(g + 1) * 512, :].rearrange("(t p) d -> p t d", p=128), xg[:, :, :],
            [("xg", t) for t in range(4)], [(dstkey, g)], "yst")

    def schedule():
        nown = cfg["own"]
        if nlayers == 1:
            L["l"] = cfg.get("layer", 0)
            prologue_layer(0)
            kv_pass(x_halo, None, O_VALID)
            for g in range(nown):
                full_group(x_own, None, g, 4 + g, None, y, "y", g if debug else None)
        else:
            L["l"] = 0
            prologue_layer(0)
            kv_pass(x_h2, None, O_VALID2)
            for e in range(4):
                full_group(x_halo, None, e, 4 + e, O_VALID, y1h, "y1h")
            for e in range(4):
                full_group(x_own, None, e, 8 + e, None, y1, "y1")
            L["l"] = 1
            prologue_layer(1)
            kv_pass(y1h, "y1h", O_VALID)
            for g in range(nown):
                full_group(y1, "y1", g, 4 + g, None, y, "y", g if debug else None)
        fw.op("sp", lambda e: None, [("y", g) for g in range(nown)]
              + ([("dbg", g) for g in range(nown)] if debug else []), [])

    fw.dry = True
    schedule()
    fw.dry = False
    wstate["next"] = 0
    bank["i"] = 0
    prologue()
    schedule()
    assert wstate["next"] == len(plan), (wstate["next"], len(plan))
    fw.emit()
    st.close()
    return nc


def _masks():
    a = np.arange(128)
    mA = np.zeros((128, 17, 128), np.float32)
    for dlt in range(17):
        dist = 128 * dlt + a[None, :] - a[:, None]
        w = ((dist >= 0) & (dist <= 128)).astype(np.float32)
        w += ((dist >= 0) & (dist <= 512) & (dist % 4 == 0)).astype(np.float32)
        w += ((dist >= 0) & (dist <= 2048) & (dist % 16 == 0)).astype(np.float32)
        mA[:, dlt, :] = w
    mC = np.zeros((128, 2, 128), np.float32)
    for dlt in range(2):
        dist = 128 * dlt + a[None, :] - a[:, None]
        mC[:, dlt, :] = ((dist >= 0) & (dist <= 127)).astype(np.float32)
    return mA.reshape(128, 17 * 128), mC.reshape(128, 2 * 128)


def _pack_cst(l, valid, valid2, norm1_g, norm2_g, a_q_g, a_k_g, conv_w, conv_b, conv_ln_g, conv_ln_b,
              c_q_g, c_k_g, c_sinks):
    c = np.zeros((128, NCST), np.float32)
    c[:, O_G1:O_G1 + 16] = norm1_g[l].reshape(16, 128).T
    c[:, O_G2:O_G2 + 16] = norm2_g[l].reshape(16, 128).T
    cw = conv_w[l].T.reshape(4, 128, 31)
    c[:, O_CW:O_CW + 124] = cw.transpose(1, 0, 2).reshape(128, 124)
    c[:, O_CB:O_CB + 4] = conv_b[l].reshape(4, 128).T
    c[:, O_LG:O_LG + 4] = conv_ln_g[l].reshape(4, 128).T
    c[:, O_LB:O_LB + 4] = conv_ln_b[l].reshape(4, 128).T
    c[:, O_VALID] = valid
    c[:, O_AQG:O_AQG + 64] = a_q_g[l][None, :]
    c[:, O_AKG:O_AKG + 64] = a_k_g[l][None, :]
    c[:, O_CQG:O_CQG + 64] = c_q_g[l][None, :]
    c[:, O_CKG:O_CKG + 64] = c_k_g[l][None, :]
    c[:, O_SINK:O_SINK + 16] = c_sinks[l][None, :]
    c[:, O_VALID2] = valid2
    return c


_PROG = {}
_RUN_KW = {}
_LAST = {}


def _get_prog(debug=False, cfg_over=None, nlayers=2):
    key = (debug, repr(cfg_over), nlayers)
    if key not in _PROG:
        _PROG[key] = build_program(debug=debug, cfg_over=cfg_over, nlayers=nlayers)
    return _PROG[key]


_VEC_KEYS = ("norm1_g", "norm2_g", "a_q_g", "a_k_g", "conv_w", "conv_b", "conv_ln_g", "conv_ln_b",
             "c_q_g", "c_k_g", "c_sinks")


def run_model(x, params, nlayers=2, debug=False, cfg_over=None, layer=0):
    cfg_over = dict(cfg_over or {})
    if nlayers == 1:
        cfg_over["layer"] = layer
    nc = _get_prog(debug, cfg_over, nlayers)
    mA, mC = _masks()
    ident = np.eye(128, dtype=np.float32)
    in_maps = []
    zeros = np.zeros((T_OWN, D), np.float32)
    for c in range(NCORES):
        b, q = divmod(c, 4)
        own = np.ascontiguousarray(x[b, q * T_OWN:(q + 1) * T_OWN])
        halo = np.ascontiguousarray(x[b, (q - 1) * T_OWN:q * T_OWN]) if q >= 1 else zeros
        h2 = np.ascontiguousarray(x[b, (q - 2) * T_OWN:(q - 1) * T_OWN]) if q >= 2 else zeros
        valid, valid2 = float(q >= 1), float(q >= 2)
        lsel = [layer, layer] if nlayers == 1 else [0, 1]
        cst = np.stack([_pack_cst(l, valid, valid2, *[params[k] for k in _VEC_KEYS]) for l in lsel])
        in_maps.append({
            "x_own": own, "x_halo": halo, "x_h2": h2,
            "w_in": params["w_in"], "w_out": params["w_out"],
            "w_gate": params["w_gate"], "w_up": params["w_up"], "w_down": params["w_down"],
            "cst": cst, "ident": ident, "maskA": mA, "maskC": mC,
        })
    res = run_bass_kernel_spmd(nc, in_maps, core_ids=list(range(NCORES)), **_RUN_KW)
    _LAST["res"] = res
    out = np.zeros_like(x)
    for c in range(NCORES):
        b, q = divmod(c, 4)
        out[b, q * T_OWN:(q + 1) * T_OWN] = res.results[c]["y"]
    if debug:
        return out, [res.results[c]["dbg"] for c in range(NCORES)]
    return out


def kernel(x, norm1_g, w_in, a_q_g, a_k_g, conv_w, conv_b, conv_ln_g, conv_ln_b,
           c_q_g, c_k_g, c_sinks, w_out, norm2_g, w_gate, w_up, w_down):
    params = dict(norm1_g=norm1_g, w_in=w_in, a_q_g=a_q_g, a_k_g=a_k_g, conv_w=conv_w, conv_b=conv_b,
                  conv_ln_g=conv_ln_g, conv_ln_b=conv_ln_b, c_q_g=c_q_g, c_k_g=c_k_g, c_sinks=c_sinks,
                  w_out=w_out, norm2_g=norm2_g, w_gate=w_gate, w_up=w_up, w_down=w_down)
    params = {k: np.ascontiguousarray(np.asarray(v, dtype=np.float32)) for k, v in params.items()}
    xx = np.asarray(x, dtype=np.float32)
    return run_model(xx, params, nlayers=2)
```

```python
from contextlib import ExitStack
import numpy as np
import concourse.bass as bass
import concourse.mybir as mybir
from concourse.bass_utils import run_bass_kernel_spmd

F32 = mybir.dt.float32
BF16 = mybir.dt.bfloat16
AF = mybir.ActivationFunctionType
ALU = mybir.AluOpType
AX = mybir.AxisListType

D = 2048
T_OWN = 2048
NG = 4
GT = 512
DFF = 5632
EPS = 1e-6
NCORES = 8
CC_ROWS = 128

C_AQ, C_AK, C_AV, C_BU, C_BG, C_CQ, C_CK = 0, 512, 1024, 1536, 2048, 2560, 3584

O_G1, O_G2 = 0, 16
O_CW = 32
O_CB = O_CW + 124
O_LG = O_CB + 4
O_LB = O_LG + 4
O_VALID = O_LB + 4
O_AQG = O_VALID + 1
O_AKG = O_AQG + 64
O_CQG = O_AKG + 64
O_CKG = O_CQG + 64
O_SINK = O_CKG + 64
O_VALID2 = O_SINK + 16
NCST = O_VALID2 + 1


class _Op:
    __slots__ = ("eng", "fn", "deps", "sig", "tick", "dma", "semval")


class FW:
    ENGS = ("pe", "act", "dve", "pool", "sp")

    def __init__(self, nc, same_engine_sync=True):
        self.nc = nc
        self.ops = {e: [] for e in self.ENGS}
        self.lastw = {}
        self.readers = {}
        self.dma_count = {}
        self.dma_last = {}
        self.ses = same_engine_sync
        self.dry = False

    def op(self, eng, fn, reads=(), writes=(), dma=None):
        if self.dry:
            return None
        o = _Op()
        o.eng, o.fn, o.deps, o.sig, o.tick, o.dma, o.semval = eng, fn, [], False, 0, dma, 0
        deps = []
        for r in reads:
            w = self.lastw.get(r)
            if w is not None:
                deps.append(w)
        for w_ in writes:
            w = self.lastw.get(w_)
            if w is not None:
                deps.append(w)
            deps.extend(self.readers.get(w_, ()))
        if dma is not None:
            prev = self.dma_last.get(dma)
            if prev is not None:
                deps.append(prev)
            self.dma_count[dma] = self.dma_count.get(dma, 0) + 1
            o.semval = 16 * self.dma_count[dma]
            self.dma_last[dma] = o
        seen = set()
        for d in deps:
            if d is o or id(d) in seen:
                continue
            seen.add(id(d))
            if d.dma is None and d.eng == eng:
                if eng == "pe":
                    continue
                if (not self.ses) and dma is None:
                    continue
            if d.dma is None:
                d.sig = True
            o.deps.append(d)
        for r in reads:
            self.readers.setdefault(r, []).append(o)
        for w_ in writes:
            self.lastw[w_] = o
            self.readers[w_] = []
        self.ops[eng].append(o)
        return o

    def emit(self):
        nc = self.nc
        for e in self.ENGS:
            t = 0
            for o in self.ops[e]:
                if o.dma is None and o.sig:
                    t += 1
                    o.tick = t
        with ExitStack() as st:
            esem = {e: st.enter_context(nc.semaphore("s_" + e)) for e in self.ENGS}
            dsem = {k: st.enter_context(nc.semaphore("d_%d" % i))
                    for i, k in enumerate(self.dma_count)}
            block = st.enter_context(nc.Block())
            ops = self.ops

            def run(e, eng):
                waited = {}
                for o in ops[e]:
                    need = {}
                    for d in o.deps:
                        if d.dma is not None:
                            key, s, v = ("d", d.dma), dsem[d.dma], d.semval
                        else:
                            key, s, v = ("e", d.eng), esem[d.eng], d.tick
                        if v > need.get(key, (None, 0))[1]:
                            need[key] = (s, v)
                    for key, (s, v) in need.items():
                        if waited.get(key, 0) >= v:
                            continue
                        eng.wait_ge(s, v)
                        waited[key] = v
                    ins = o.fn(eng)
                    if ins is None:
                        continue
                    if o.dma is not None:
                        ins.then_inc(dsem[o.dma], 16)
                    elif o.sig:
                        ins.then_inc(esem[e], 1)

            @block.tensor
            def _(eng):
                run("pe", eng)

            @block.scalar
            def _(eng):
                run("act", eng)

            @block.vector
            def _(eng):
                run("dve", eng)

            @block.gpsimd
            def _(eng):
                run("pool", eng)

            @block.sync
            def _(eng):
                run("sp", eng)


FFN_SEGS = [[0, 1, 2], [3, 4, 5], [6, 7, 8], [9, 10]]


def build_program(debug=False, cfg_over=None, same_engine_sync=True, nlayers=2):
    nc = bass.Bass("TRN2", target_bir_lowering=False)
    dt_in = lambda name, shape: nc.dram_tensor(name, shape, F32, kind="ExternalInput").ap()
    x_own = dt_in("x_own", [T_OWN, D])
    x_halo = dt_in("x_halo", [T_OWN, D])
    x_h2 = dt_in("x_h2", [T_OWN, D])
    w_in_all = dt_in("w_in", [2, D, 3840])
    w_out_all = dt_in("w_out", [2, D, D])
    w_gate_all = dt_in("w_gate", [2, D, DFF])
    w_up_all = dt_in("w_up", [2, D, DFF])
    w_down_all = dt_in("w_down", [2, DFF, D])
    cst_all = dt_in("cst", [2, 128, NCST])
    y1 = nc.dram_tensor("y1", [T_OWN, D], F32).ap()
    y1h = nc.dram_tensor("y1h", [T_OWN, D], F32).ap()
    wc = nc.dram_tensor("wc", [104, 128, 8192], BF16).ap()
    wc_slot = {}
    L = {"l": 0}
    ident_d = dt_in("ident", [128, 128])
    maskA_d = dt_in("maskA", [128, 17 * 128])
    maskC_d = dt_in("maskC", [128, 2 * 128])
    y = nc.dram_tensor("y", [T_OWN, D], F32, kind="ExternalOutput").ap()
    kt_d = nc.dram_tensor("kt_d", [4, 128, 12 * 512], BF16).ap()
    v_d = nc.dram_tensor("v_d", [4, 128, 48, 130], BF16).ap()
    dbg = None
    if debug:
        dbg = nc.dram_tensor("dbg", [NG, 128, 16, GT], BF16, kind="ExternalOutput").ap()

    cfg = dict(ipstop=99, halo=4, own=NG, attnA=True, conv=True, attnC=True, outp=True, ffn=True, inproj=True, kvdram=True)
    cfg.update(cfg_over or {})
    fw = FW(nc, same_engine_sync=same_engine_sync)
    st = ExitStack()
    sb = lambda name, shape, dt: st.enter_context(nc.sbuf_tensor(name, shape, dt))
    ps = lambda name, shape, dt: st.enter_context(nc.psum_tensor(name, shape, dt))

    xg = sb("xg", [128, 4, D], F32)
    xn = sb("xn", [128, D], BF16)
    hT = sb("hT", [128, 16, GT], BF16)
    Wb = [sb("W%d" % i, [128, 16, 512], BF16) for i in range(3)]
    sq2 = [sb("sq0", [128, 512], F32), sb("sq1", [128, 512], F32)]
    qn2 = [sb("qn%d" % i, [128, 512], BF16) for i in range(3)]
    QTA = sb("QTA", [128, 2, 4, GT], BF16)
    QTC = sb("QTC", [128, 8, GT], BF16)
    KTs = sb("KTs", [128, 4, GT], BF16)
    Vs = sb("Vs", [128, 4, 8, 65], BF16)
    KTb = sb("KTb", [128, 2560], BF16)
    Vb = sb("Vb", [128, 20, 130], BF16)
    ckd2 = [sb("ckd%d" % i, [128, 2, 2, 64], BF16) for i in range(2)]
    KTC = sb("KTC", [128, 2, 640], BF16)
    VC = sb("VC", [128, 5, 2, 65], BF16)
    hg = sb("hg", [128, 512], F32)
    hgT = sb("hgT", [128, 4, 544], F32)
    cbuf = sb("cbuf", [128, 3072], F32)
    yT = cbuf[:, 0:2048].rearrange("p (c t) -> p c t", t=512)
    lnA = cbuf[:, 2048:2560]
    lnB = cbuf[:, 2560:3072]
    lnC = sb("lnC", [128, 512], F32)
    Eb = [sb("E%d" % i, [128, 512], BF16) for i in range(4)]
    Pb = [sb("P%d" % i, [128, 512], BF16) for i in range(4)]
    mixtok = sb("mixtok", [128, 4, 128], BF16)
    mixT = sb("mixT", [128, 16, GT], BF16)
    actT = cbuf[:, :].bitcast(BF16).rearrange("p (k t) -> p k t", t=GT)
    sg = sb("sg", [128, 512], F32)
    maskA = sb("maskA_s", [128, 17 * 128], BF16)
    maskC = sb("maskC_s", [128, 2 * 128], BF16)
    ident_f = sb("ident_f", [128, 128], F32)
    ident_b = sb("ident_b", [128, 128], BF16)
    cst = sb("cst_s", [128, NCST], F32)
    zeros_b = sb("zeros_b", [128, 512], BF16)
    onesm = sb("onesm", [128, 128], F32)
    small = sb("small", [128, 128], F32)
    es = sb("es", [128, 16], F32)
    epsb = sb("epsb", [128, 1], F32)

    pmm = [ps("pmm%d" % i, [128, 512], F32) for i in range(4)]
    ptr = [ps("ptr%d" % i, [128, 1024], BF16) for i in range(2)]
    pf = [ps("pf%d" % i, [128, 512], F32) for i in range(2)]

    S_SS, S_MS, S_SD, S_RS = 0, 4, 8, 12
    S_H = 16
    S_DEN, S_REC = 48, 56
    S_H2 = 64

    def act_op(out, in_, func, reads, writes, **kw):
        return fw.op("act", lambda e, o=out, i=in_, f=func, kw=kw: e.activation(o, i, f, **kw), reads, writes)

    def tt(out, in0, in1, op, reads, writes, eng="dve"):
        return fw.op(eng, lambda e, o=out, a=in0, b=in1, p=op: e.tensor_tensor(o, a, b, p), reads, writes)

    def tsc(out, in0, s1, s2, op0, op1, reads, writes, eng="dve"):
        if op1 is None:
            return fw.op(eng, lambda e, o=out, a=in0, s1=s1, p0=op0: e.tensor_scalar(o, a, s1, None, p0), reads, writes)
        return fw.op(eng, lambda e, o=out, a=in0, s1=s1, s2=s2, p0=op0, p1=op1:
                     e.tensor_scalar(o, a, s1, s2, p0, p1), reads, writes)

    def copy(eng, out, in_, reads, writes):
        if eng == "act":
            return fw.op("act", lambda e, o=out, i=in_: e.copy(o, i), reads, writes)
        return fw.op(eng, lambda e, o=out, i=in_: e.tensor_copy(o, i), reads, writes)

    def recip(out, in_, reads, writes):
        return fw.op("dve", lambda e, o=out, i=in_: e.reciprocal(o, i), reads, writes)

    def mm(out, lhsT, rhs, start, stop, reads, writes):
        return fw.op("pe", lambda e, o=out, l=lhsT, r=rhs, s=start, t=stop:
                     e.matmul(o, l, r, start=s, stop=t), reads, writes)

    def trn(out, in_, identity, reads, writes):
        return fw.op("pe", lambda e, o=out, i=in_, d=identity: e.transpose(o, i, d), reads, writes)

    def dma(q, out, in_, reads, writes, key):
        return fw.op(q, lambda e, o=out, i=in_: e.dma_start(out=o, in_=i), reads, writes, dma=key)

    def rstd_from_ss(ncol, c_ss, c_ms, c_sd, c_rs, inv_n, tag):
        r = ("small", tag)
        tsc(small[:, c_ms:c_ms + ncol], small[:, c_ss:c_ss + ncol], inv_n, EPS, ALU.mult, ALU.add, [r], [r])
        act_op(small[:, c_sd:c_sd + ncol], small[:, c_ms:c_ms + ncol], AF.Sqrt, [r], [r])
        recip(small[:, c_rs:c_rs + ncol], small[:, c_sd:c_sd + ncol], [r], [r])

    plan = []
    wstate = {"loaded": 0, "next": 0, "released": set()}

    def _wsrc(req):
        kind = req[0]
        l = req[-1]
        req = req[:-1]
        w_in, w_out, w_gate, w_up, w_down = (w_in_all[l], w_out_all[l], w_gate_all[l], w_up_all[l], w_down_all[l])
        if kind == "in":
            _, c0, n = req
            return w_in[:, c0:c0 + n].rearrange("(k p) c -> p k c", p=128), 16, n
        if kind == "out":
            n = req[1]
            return w_out[:, n * 512:(n + 1) * 512].rearrange("(k p) c -> p k c", p=128), 16, 512
        if kind == "gate":
            b = req[1]
            return w_gate[:, b * 512:(b + 1) * 512].rearrange("(k p) c -> p k c", p=128), 16, 512
        if kind == "up":
            b = req[1]
            return w_up[:, b * 512:(b + 1) * 512].rearrange("(k p) c -> p k c", p=128), 16, 512
        if kind == "down":
            _, c0, nch, n = req
            return (w_down[c0 * 128:(c0 + nch) * 128, n * 512:(n + 1) * 512]
                    .rearrange("(k p) c -> p k c", p=128), nch, 512)
        raise ValueError(kind)

    def _wpump():
        while wstate["loaded"] < len(plan):
            i = wstate["loaded"]
            if i >= 3 and (i - 3) not in wstate["released"]:
                break
            src, nk, n = _wsrc(plan[i])
            s = i % 3
            key = plan[i]
            if key in wc_slot:
                sl = wc_slot[key]
                dma("pool", Wb[s][:, 0:nk, 0:n], wc[sl, :, 0:nk * n].rearrange("p (k c) -> p k c", c=n),
                    [("wc", sl)], [("W", s)], ("W", s))
            else:
                sl = len(wc_slot)
                wc_slot[key] = sl
                dma("pool", Wb[s][:, 0:nk, 0:n], src, [], [("W", s)], ("W", s))
                dma("sp", wc[sl, :, 0:nk * n].rearrange("p (k c) -> p k c", c=n), Wb[s][:, 0:nk, 0:n],
                    [("W", s)], [("wc", sl)], ("wcst", s))
            wstate["loaded"] += 1

    def wget(*req):
        req = req + (L["l"],)
        i = wstate["next"]
        wstate["next"] += 1
        if fw.dry:
            plan.append(req)
            return i, i % 3
        assert plan[i] == req, (i, plan[i], req)
        _wpump()
        assert wstate["loaded"] > i
        return i, i % 3

    def wrel(i):
        if fw.dry:
            return
        wstate["released"].add(i)
        _wpump()

    def prologue_layer(l):
        dma("sp", cst[:, :], cst_all[l], [], ["cst"], "cst")
        act_op(es[:, :], cst[:, O_SINK:O_SINK + 16], AF.Exp, ["cst"], ["es"])
        tt(cst[:, O_AKG:O_AKG + 64], cst[:, O_AKG:O_AKG + 64], cst[:, O_AQG:O_AQG + 64], ALU.mult, ["cst"], ["cst"])
        tt(cst[:, O_CKG:O_CKG + 64], cst[:, O_CKG:O_CKG + 64], cst[:, O_CQG:O_CQG + 64], ALU.mult, ["cst"], ["cst"])

    def prologue():
        dma("sp", ident_f[:, :], ident_d, [], ["ident_f"], "identf")
        dma("pool", ident_b[:, :], ident_d, [], ["ident_b"], "identb")
        dma("pool", maskA[:, 0:1024], maskA_d[:, 0:1024], [], ["maskA"], "maskA")
        dma("pool", maskA[:, 1024:2176], maskA_d[:, 1024:2176], [], ["maskA"], "maskA")
        dma("pool", maskC[:, :], maskC_d, [], ["maskC"], "maskC")
        fw.op("dve", lambda e: e.memset(zeros_b[:, :], 0.0), [], ["zeros"])
        fw.op("dve", lambda e: e.memset(onesm[:, :], 1.0 / 512.0), [], ["onesm"])
        fw.op("dve", lambda e: e.memset(epsb[:, :], EPS), [], ["epsb"])
        fw.op("dve", lambda e: e.memset(QTA[:, :, :, :], 0.0), [], ["QTA"])

    def load_and_norm(src_rows, gcol, load):
        for t in range(4):
            if load:
                src_rows(t)
            act_op(xn[:, :], xg[:, t, :], AF.Square, [("xg", t)], [("xn", 0), ("xn", 1), ("small", "n")],
                   accum_out=small[:, S_SS + t:S_SS + t + 1])
        rstd_from_ss(4, S_SS, S_MS, S_SD, S_RS, 1.0 / D, "n")
        for t in range(4):
            for half in range(2):
                act_op(xn[:, half * 1024:(half + 1) * 1024], xg[:, t, half * 1024:(half + 1) * 1024], AF.Copy,
                       [("xg", t), ("small", "n")], [("xn", half)], scale=small[:, S_RS + t:S_RS + t + 1])
            for half in range(2):
                pt = ptr[half]
                for j in range(8):
                    k = half * 8 + j
                    trn(pt[:, j * 128:(j + 1) * 128], xn[:, k * 128:(k + 1) * 128], ident_b[:, :],
                        [("xn", half), "ident_b"], [("ptr", half)])
                for j in range(8):
                    k = half * 8 + j
                    dst = hT[:, k, t * 128:(t + 1) * 128]
                    src = pt[:, j * 128:(j + 1) * 128]
                    gap = cst[:, gcol + k:gcol + k + 1]
                    tsc(dst, src, gap, None, ALU.mult, None, [("ptr", half), "cst"], [("hT", k)])

    bank = {"i": 0}
    deferred = []

    def defer(fn):
        deferred.append(fn)

    def flush_deferred(keep=0):
        while len(deferred) > keep:
            deferred.pop(0)()

    def proj_tile(s, t, ncols, nk=16, lhs=None, lhs_key="hT"):
        keyed = lhs_key in ("hT", "actT")
        lhs = hT if lhs is None else lhs
        b = bank["i"] % 4
        bank["i"] += 1
        for k in range(nk):
            mm(pmm[b][:, 0:ncols], lhs[:, k, t * 128:(t + 1) * 128], Wb[s][:, k, 0:ncols],
               k == 0, k == nk - 1, [(lhs_key, k) if keyed else lhs_key, ("W", s)], [("pmm", b)])
        flush_deferred(keep=1)
        return pmm[b], ("pmm", b)

    qkc = {"i": 0}

    def qk_norm(psrc, pkey, nh, gcol):
        bi = qkc["i"] % 2
        bq = qkc["i"] % 3
        qkc["i"] += 1
        sq, qn = sq2[bi], qn2[bq]
        ks, kn, kh = ("sq", bi), ("qn", bq), ("small", "h", bi)
        so = S_H if bi == 0 else S_H2
        w = nh * 64
        act_op(sq[:, 0:w], psrc[:, 0:w], AF.Square, [pkey], [ks])
        fw.op("dve", lambda e, nh=nh, w=w, sq=sq, so=so: e.tensor_reduce(
            small[:, so:so + nh], sq[:, 0:w].rearrange("p (h d) -> p h d", d=64), AX.X, ALU.add),
            [ks], [kh])
        if cfg.get("rsq", False):
            act_op(small[:, so + 24:so + 32], small[:, so:so + 8], AF.Abs_reciprocal_sqrt, [kh, "epsb"], [kh],
                   scale=1.0 / 64.0, bias=epsb[:, 0:1])
        else:
            tsc(small[:, so + 8:so + 16], small[:, so:so + 8], 1.0 / 64.0, EPS, ALU.mult, ALU.add, [kh], [kh])
            act_op(small[:, so + 16:so + 24], small[:, so + 8:so + 16], AF.Sqrt, [kh], [kh])
            recip(small[:, so + 24:so + 32], small[:, so + 16:so + 24], [kh], [kh])
        rs_b = small[:, so + 24:so + 24 + nh].unsqueeze(2).to_broadcast([128, nh, 64])
        p3 = psrc[:, 0:w].rearrange("p (h d) -> p h d", d=64)
        if gcol is None:
            tt(qn[:, 0:w].rearrange("p (h d) -> p h d", d=64), p3, rs_b, ALU.mult, [pkey, kh], [kn])
        else:
            tt(sq[:, 0:w].rearrange("p (h d) -> p h d", d=64), p3, rs_b, ALU.mult, [pkey, kh], [ks])
            g_b = cst[:, gcol:gcol + 64].unsqueeze(1).to_broadcast([128, nh, 64])
            tt(qn[:, 0:w].rearrange("p (h d) -> p h d", d=64), sq[:, 0:w].rearrange("p (h d) -> p h d", d=64),
               g_b, ALU.mult, [ks, "cst"], [kn])
        return None, ks, qn, kn

    def transpose_pairs(src, src_key, npairs, dst3, t, dst_key, half):
        pt = ptr[half]
        for p in range(npairs):
            trn(pt[:, p * 128:(p + 1) * 128], src[:, p * 128:(p + 1) * 128], ident_b[:, :],
                [src_key, "ident_b"], [("ptr", half)])
        copy("act" if half else "dve", dst3[:, 0:npairs, t * 128:(t + 1) * 128],
             pt[:, 0:npairs * 128].rearrange("p (a b) -> p a b", b=128), [("ptr", half)], [dst_key])

    def in_proj_group(G, kvonly, kv_last, hist, vcol):
        is_halo = vcol is not None
        last_halo = kv_last
        own = not kvonly
        vap = cst[:, vcol:vcol + 1] if vcol is not None else None
        ips = cfg.get("ipstop", 99)
        if own:
            i, s = wget("in", C_AQ, 512)
            for t in range(4):
                if ips >= 2:
                    pp, pk = proj_tile(s, t, 512)
                if ips >= 3:
                    _, _, qn, kn = qk_norm(pp, pk, 8, None)
                if ips >= 4:
                    def aq_tr(qn=qn, kn=kn, t=t):
                        pt = ptr[t % 2]
                        for p in range(4):
                            trn(pt[:, p * 128:(p + 1) * 128], qn[:, p * 128:(p + 1) * 128], ident_b[:, :],
                                [kn, "ident_b"], [("ptr", t % 2)])
                        src3 = pt[:, 0:512].rearrange("p (a b) -> p a b", b=128)
                        copy("dve", QTA[0:64, 0, :, t * 128:(t + 1) * 128], src3[0:64], [("ptr", t % 2)], ["QTA"])
                        copy("act", QTA[64:128, 1, :, t * 128:(t + 1) * 128], src3[64:128], [("ptr", t % 2)], ["QTA"])
                    defer(aq_tr)
            wrel(i)
        if ips <= 4:
            flush_deferred()
            return
        i, s = wget("in", C_AK, 512)
        for t in range(4):
            pp, pk = proj_tile(s, t, 512)
            _, _, qn, kn = qk_norm(pp, pk, 8, O_AKG)
            defer(lambda qn=qn, kn=kn, t=t: transpose_pairs(qn, kn, 4, KTs, t, "KTs", t % 2))
        wrel(i)
        flush_deferred()
        if cfg["kvdram"]:
            dma("sp", kt_d[:, :, G * 512:(G + 1) * 512].rearrange("a q t -> q a t"), KTs[:, :, :],
                ["KTs"], [("kt_d", G)], "ktw")
        if ips <= 5:
            return
        i, s = wget("in", C_AV, 512)
        if is_halo:
            tsc(Vs[:, :, :, 64:65], zeros_b[:, 0:32].rearrange("p (a b c) -> p a b c", a=4, b=8),
                1.0, vap, ALU.add, ALU.mult, ["zeros", "cst"], ["Vs"])
        else:
            tsc(Vs[:, :, :, 64:65], zeros_b[:, 0:32].rearrange("p (a b c) -> p a b c", a=4, b=8),
                1.0, None, ALU.add, None, ["zeros"], ["Vs"])
        for t in range(4):
            pp, pk = proj_tile(s, t, 512)
            src = pp[:, 0:512].rearrange("p (h d) -> p h d", d=64)
            if is_halo:
                tsc(Vs[:, t, :, 0:64], src, vap, None, ALU.mult, None,
                    [pk, "cst"], ["Vs"])
            else:
                copy("dve", Vs[:, t, :, 0:64], src, [pk], ["Vs"])
        wrel(i)
        for a in range(4 if cfg["kvdram"] else 0):
            dma("sp", v_d[a, :, 4 * G:4 * G + 4, :],
                Vs[:, :, 2 * a:2 * a + 2, :].rearrange("p t b c -> p t (b c)"), ["Vs"], [("v_d", G, a)], "vw")
        if kvonly and not last_halo:
            flush_deferred()
            return
        if ips <= 6:
            return
        if hist:
            copy("dve", hgT[:, :, 0:32], hgT[:, :, 512:544], ["hgT"], ["hgT"])
        iu, su = wget("in", C_BU, 512)
        ig, sg_ = wget("in", C_BG, 512)
        for t in range(4):
            pu, ku = proj_tile(su, t, 512)
            pg, kg = proj_tile(sg_, t, 512)
            hgb, hk = (hg, "hg") if t % 2 == 0 else (sq2[0], ("sq", 0))
            act_op(hgb[:, :], pg[:, :], AF.Sigmoid, [kg], [hk])
            tt(hgb[:, :], hgb[:, :], pu[:, :], ALU.mult, [hk, ku], [hk])
            if vap is not None:
                tsc(hgb[:, :], hgb[:, :], vap, None, ALU.mult, None, [hk, "cst"], [hk])
            def hg_tr(t=t, hgb=hgb, hk=hk):
                pfb = pf[t % 2]
                for c in range(4):
                    trn(pfb[:, c * 128:(c + 1) * 128], hgb[:, c * 128:(c + 1) * 128], ident_f[:, :],
                        [hk, "ident_f"], [("pf", t % 2)])
                copy("act", hgT[:, :, 32 + t * 128:32 + (t + 1) * 128],
                     pfb[:, :].rearrange("p (c b) -> p c b", b=128), [("pf", t % 2)], ["hgT"])
            defer(hg_tr)
        wrel(iu)
        wrel(ig)
        flush_deferred()
        if ips <= 7:
            return
        if own:
            for half in range(2):
                i, s = wget("in", C_CQ + half * 512, 512)
                for t in range(4):
                    pp, pk = proj_tile(s, t, 512)
                    _, _, qn, kn = qk_norm(pp, pk, 8, None)

                    def cq_tr(qn=qn, kn=kn, t=t, half=half):
                        pt = ptr[t % 2]
                        for p in range(4):
                            trn(pt[:, p * 128:(p + 1) * 128], qn[:, p * 128:(p + 1) * 128], ident_b[:, :],
                                [kn, "ident_b"], [("ptr", t % 2)])
                        copy("act" if t % 2 else "dve", QTC[:, half * 4:half * 4 + 4, t * 128:(t + 1) * 128],
                             pt[:, 0:512].rearrange("p (a b) -> p a b", b=128), [("ptr", t % 2)], ["QTC"])
                    defer(cq_tr)
                wrel(i)
        if ips <= 8:
            return
        if hist:
            copy("dve", KTC[:, :, 0:128], KTC[:, :, 512:640], ["KTC"], ["KTC"])
            copy("dve", VC[:, 0, :, :], VC[:, 4, :, :], ["VC"], ["VC"])
        i, s = wget("in", C_CK, 256)
        vscal = vap
        if is_halo:
            tsc(VC[:, 1:5, :, 64:65], zeros_b[:, 0:8].rearrange("p (a b c) -> p a b c", a=4, b=2),
                1.0, vscal, ALU.add, ALU.mult, ["zeros", "cst"], ["VC"])
        else:
            tsc(VC[:, 1:5, :, 64:65], zeros_b[:, 0:8].rearrange("p (a b c) -> p a b c", a=4, b=2),
                1.0, None, ALU.add, None, ["zeros"], ["VC"])
        cs = cfg.get("cstop", 99)
        for t in range(4):
            if cs < 1:
                continue
            pp, pk = proj_tile(s, t, 256)
            vsrc = pp[:, 128:256].rearrange("p (h d) -> p h d", d=64)
            _, ks_, qn, kn = qk_norm(pp, pk, 4, O_CKG)
            vsrc = pp[:, 128:256].rearrange("p (h d) -> p h d", d=64)
            if is_halo:
                tsc(VC[:, 1 + t, :, 0:64], vsrc, vscal, None, ALU.mult, None, [pk, kn, "cst"], ["VC"])
            else:
                copy("dve", VC[:, 1 + t, :, 0:64], vsrc, [pk, kn], ["VC"])
            if cs < 2:
                continue
            if cs < 3:
                continue
            copy("dve", ckd2[t % 2][:, :, :, :],
                 qn[:, 0:128].rearrange("p (h d) -> p h d", d=64).unsqueeze(2).to_broadcast([128, 2, 2, 64]),
                 [kn], [("ckd", t % 2)])
            def ck_tr(t=t):
                pt = ptr[t % 2]
                for kv in range(2):
                    trn(pt[:, kv * 128:(kv + 1) * 128], ckd2[t % 2][:, kv, :, :].rearrange("p a d -> p (a d)"),
                        ident_b[:, :], [("ckd", t % 2), "ident_b"], [("ptr", t % 2)])
                copy("act" if t % 2 else "dve", KTC[:, :, 128 + t * 128:128 + (t + 1) * 128],
                     pt[:, 0:256].rearrange("p (a b) -> p a b", b=128), [("ptr", t % 2)], ["KTC"])
            defer(ck_tr)
        wrel(i)
        flush_deferred()

    sbanks = [pmm[0], pmm[1], pf[0], pf[1]]
    skeys = [("pmm", 0), ("pmm", 1), ("pf", 0), ("pf", 1)]
    ucnt = {"i": 0}
    LA = 3

    def attention_pair(Kap, Vap, kkey, vkey, Qt, qkey, p, nkt, off, R, mask, mkey, sink_cols, filler):
        for hl in range(2):
            mm(pmm[2 + hl][:, 0:260], zeros_b[:, 0:128], zeros_b[:, 0:260], True, False, ["zeros"], [("pmm", 2 + hl)])
        units = []
        for j in range(nkt):
            i_lo = max(0, j - off)
            i_hi = min(3, j - off + R)
            if i_hi < i_lo:
                continue
            for hl in range(2):
                units.append((j, hl, i_lo, i_hi))
        n = len(units)
        lastu = {hl: max(u for u in range(n) if units[u][1] == hl) for hl in range(2)}
        slots = {}

        def front(u):
            j, hl, i_lo, i_hi = units[u]
            hh = hl * 64
            b = ucnt["i"] % 4
            ucnt["i"] += 1
            slots[u] = b
            nq = i_hi - i_lo + 1
            d_lo = off + i_lo - j
            psb = sbanks[b]
            mm(psb[:, 0:nq * 128], Kap(hh, j), Qt(hl, hh, i_lo * 128, (i_hi + 1) * 128),
               True, True, [kkey, qkey], [skeys[b]])
            act_op(Eb[b][:, 0:nq * 128], psb[:, 0:nq * 128], AF.Exp, [skeys[b]], [("E", b)], scale=0.125)
            tt(Pb[b][:, 0:nq * 128], Eb[b][:, 0:nq * 128], mask[:, d_lo * 128:(d_lo + nq) * 128],
               ALU.mult, [("E", b), mkey], [("P", b)])
            if filler is not None:
                filler()
            if u == 4:
                flush_deferred()

        def back(u):
            j, hl, i_lo, i_hi = units[u]
            b = slots[u]
            po = pmm[2 + hl]
            for i in range(i_lo, i_hi + 1):
                last = (u == lastu[hl]) and (i == i_hi)
                mm(po[:, i * 65:(i + 1) * 65], Pb[b][:, (i - i_lo) * 128:(i - i_lo + 1) * 128], Vap(hl, j),
                   False, last, [("P", b), vkey], [("pmm", 2 + hl)])

        for idx in range(n + LA):
            if idx < n:
                front(idx)
            if idx >= LA:
                back(idx - LA)
        for hl in range(2):
            po = pmm[2 + hl]
            okey = ("pmm", 2 + hl)
            po3 = po[:, 0:260].rearrange("p (i c) -> p i c", c=65)
            den = small[:, S_DEN + 4 * hl:S_DEN + 4 * hl + 4]
            rec = small[:, S_REC + 4 * hl:S_REC + 4 * hl + 4]
            ka = ("small", "a", hl)
            if sink_cols is None:
                tsc(den, po3[:, :, 64:65].rearrange("p i c -> p (i c)"), 1e-30, None, ALU.max, None, [okey], [ka])
            else:
                sc = sink_cols[hl]
                tsc(den, po3[:, :, 64:65].rearrange("p i c -> p (i c)"), es[:, sc:sc + 1], None, ALU.add, None,
                    [okey, "es"], [ka])
            recip(rec, den, [ka], [ka])
            tt(mixtok[:, :, hl * 64:(hl + 1) * 64], po3[:, :, 0:64],
               rec.unsqueeze(2).to_broadcast([128, 4, 64]), ALU.mult, [okey, ka], ["mixtok"])

    def mixtok_to_mixT(chunk, half):
        pt = ptr[half]
        for i in range(4):
            trn(pt[:, i * 128:(i + 1) * 128], mixtok[:, i, :], ident_b[:, :], ["mixtok", "ident_b"], [("ptr", half)])
        copy("act" if half else "dve", mixT[:, chunk, :], pt[:, 0:512], [("ptr", half)], ["mixT"])

    def attn_A(g, filler=None):
        for p in range(4):
            dma("sp", KTb[:, :], kt_d[p, :, 4 * g * 128:(4 * g + 20) * 128],
                [("kt_d", G) for G in range(g, g + 5)], ["KTb"], "ktb")
            dma("sp", Vb[:, :, :], v_d[p, :, 4 * g:4 * g + 20, :],
                [("v_d", G, p) for G in range(g, g + 5)], ["Vb"], "vb")
            attention_pair(lambda hh, j: KTb[:, j * 128:(j + 1) * 128],
                           lambda hl, j: Vb[:, j, hl * 65:(hl + 1) * 65],
                           "KTb", "Vb", (lambda hl, hh, c0, c1, p=p: QTA[:, hl, p, c0:c1]), "QTA",
                           p, 20, 16, 16, maskA, "maskA", None, filler)
            defer(lambda p=p: mixtok_to_mixT(p, p % 2))
        flush_deferred()

    def attn_C(filler=None):
        for p in range(8):
            kv = (2 * p) // 8
            attention_pair(lambda hh, j, kv=kv: KTC[hh:hh + 64, kv, j * 128:(j + 1) * 128],
                           lambda hl, j, kv=kv: VC[:, j, kv, :],
                           "KTC", "VC", (lambda hl, hh, c0, c1, p=p: QTC[hh:hh + 64, p, c0:c1]), "QTC",
                           p, 5, 1, 1, maskC, "maskC", (2 * p, 2 * p + 1), filler)
            defer(lambda p=p: mixtok_to_mixT(8 + p, p % 2))
        flush_deferred()

    def conv_ops():
        ops = []
        wcol = lambda c, j: cst[:, O_CW + c * 31 + j:O_CW + c * 31 + j + 1]
        for c in range(4):
            ops.append(lambda c=c: tsc(yT[:, c, :], hgT[:, c, 2:514], wcol(c, 0), cst[:, O_CB + c:O_CB + c + 1],
                                       ALU.mult, ALU.add, ["hgT", "cst"], [("yT", c)]))
        for j in range(1, 31):
            for c in range(4):
                ops.append(lambda c=c, j=j, wc=wcol(c, j): fw.op("dve", lambda e: e.scalar_tensor_tensor(
                    yT[:, c, :], hgT[:, c, 2 + j:514 + j], wc, yT[:, c, :], ALU.mult, ALU.add),
                    ["hgT", "cst", ("yT", c)], [("yT", c)]))
        return ops

    def ln_part():
        for c in range(4):
            mm(pf[0][:, :], onesm[:, :], yT[:, c, :], c == 0, c == 3, ["onesm", ("yT", c)], [("pf", 0)])
        for c in range(4):
            buf, bk = (lnA, "lnA") if c % 2 == 0 else (lnB, "lnB")
            act_op(buf[:, :], yT[:, c, :], AF.Square, [("yT", c)], [bk])
            mm(pf[1][:, :], onesm[:, :], buf[:, :], c == 0, c == 3, ["onesm", bk], [("pf", 1)])
        copy("act", lnC[:, :], pf[0][:, :], [("pf", 0)], ["lnC"])
        tt(lnA[:, :], lnC[:, :], lnC[:, :], ALU.mult, ["lnC"], ["lnA"])
        tt(lnA[:, :], pf[1][:, :], lnA[:, :], ALU.subtract, [("pf", 1), "lnA"], ["lnA"])
        tsc(lnA[:, :], lnA[:, :], EPS, None, ALU.add, None, ["lnA"], ["lnA"])
        act_op(lnA[:, :], lnA[:, :], AF.Sqrt, ["lnA"], ["lnA"])
        recip(lnA[:, :], lnA[:, :], ["lnA"], ["lnA"])
        for c in range(4):
            tt(lnB[:, :], yT[:, c, :], lnC[:, :], ALU.subtract, [("yT", c), "lnC"], ["lnB"])
            tt(lnB[:, :], lnB[:, :], lnA[:, :], ALU.mult, ["lnB", "lnA"], ["lnB"])
            act_op(mixT[:, 4 + c, :], lnB[:, :], AF.Silu, ["lnB", "cst"], ["mixT"],
                   scale=cst[:, O_LG + c:O_LG + c + 1], bias=cst[:, O_LB + c:O_LB + c + 1])

    def out_proj():
        for n in range(4):
            i, s = wget("out", n)
            for t in range(4):
                pp, pk = proj_tile(s, t, 512, lhs=mixT, lhs_key="mixT")
                tt(xg[:, t, n * 512:(n + 1) * 512], xg[:, t, n * 512:(n + 1) * 512], pp[:, :], ALU.add,
                   [("xg", t), pk], [("xg", t)])
            wrel(i)

    def ffn():
        fw.op("dve", lambda e: e.memset(small[:, 101:102], 0.0), [],
              [("yT", 0), ("yT", 1), ("yT", 2), ("yT", 3), "lnA", "lnB", ("small", "f2")])
        for seg in FFN_SEGS:
            for lb, b in enumerate(seg):
                ig, sgs = wget("gate", b)
                iu, sus = wget("up", b)
                for f in range(4):
                    bg_, bu_ = (0, 1) if f % 2 == 0 else (2, 3)
                    for k in range(16):
                        mm(pmm[bg_][:, :], Wb[sgs][:, k, f * 128:(f + 1) * 128], hT[:, k, :], k == 0, k == 15,
                           [("W", sgs), ("hT", k)], [("pmm", bg_)])
                    for k in range(16):
                        mm(pmm[bu_][:, :], Wb[sus][:, k, f * 128:(f + 1) * 128], hT[:, k, :], k == 0, k == 15,
                           [("W", sus), ("hT", k)], [("pmm", bu_)])
                    act_op(sg[:, :], pmm[bg_][:, :], AF.Silu, [("pmm", bg_)], ["sg"])
                    tt(actT[:, lb * 4 + f, :], sg[:, :], pmm[bu_][:, :], ALU.mult, ["sg", ("pmm", bu_)],
                       [("actT", lb * 4 + f)])
                wrel(ig)
                wrel(iu)
            nch = 4 * len(seg)
            for n in range(4):
                i, s = wget("down", seg[0] * 4, nch, n)
                for t in range(4):
                    pp, pk = proj_tile(s, t, 512, nk=nch, lhs=actT, lhs_key="actT")
                    tt(xg[:, t, n * 512:(n + 1) * 512], xg[:, t, n * 512:(n + 1) * 512], pp[:, :], ALU.add,
                       [("xg", t), pk], [("xg", t)])
                wrel(i)


    def direct_loader(src, g, rkeys):
        def f(t):
            dma("sp", xg[:, t, :], src[g * 512 + t * 128:g * 512 + (t + 1) * 128, :],
                rkeys, [("xg", t)], ("xg", t))
        return f

    def kv_pass(src, srckey, vcol):
        for hg in range(4 - cfg["halo"], 4):
            load_and_norm(direct_loader(src, hg, [(srckey, hg)] if srckey else []), O_G1, True)
            if cfg["inproj"]:
                in_proj_group(hg, True, hg == 3, False, vcol)

    def full_group(src, srckey, g, slot, vcol, dst, dstkey, dbg_g=None):
        load_and_norm(direct_loader(src, g, [(srckey, g)] if srckey else []), O_G1, True)
        if cfg["inproj"]:
            in_proj_group(slot, False, False, True, vcol)
        fw.op("dve", lambda e: e.memset(small[:, 100:101], 0.0), [], [("actT", k) for k in range(12)] + [("small", "f")])
        pend = conv_ops() if cfg["conv"] else []
        fcnt = {"i": 0}

        def filler():
            fcnt["i"] += 1
            if pend and fcnt["i"] % 2 == 0:
                pend.pop(0)()
        if cfg["attnA"]:
            attn_A(slot - 4, filler)
        if cfg["attnC"]:
            attn_C(filler)
        while pend:
            pend.pop(0)()
        if cfg["conv"]:
            ln_part()
        if dbg_g is not None:
            dma("sp", dbg[dbg_g], mixT[:, :, :], ["mixT"], [("dbg", dbg_g)], "dbg")
        if cfg["outp"]:
            out_proj()
        if cfg["ffn"]:
            load_and_norm(None, O_G2, False)
            ffn()
        dma("sp", dst[g * 512:(g + 1) * 512, :].rearrange("(t p) d -> p t d", p=128), xg[:, :, :],
            [("xg", t) for t in range(4)], [(dstkey, g)], "yst")

    def schedule():
        nown = cfg["own"]
        if nlayers == 1:
            L["l"] = cfg.get("layer", 0)
            prologue_layer(0)
            kv_pass(x_halo, None, O_VALID)
            for g in range(nown):
                full_group(x_own, None, g, 4 + g, None, y, "y", g if debug else None)
        else:
            L["l"] = 0
            prologue_layer(0)
            kv_pass(x_h2, None, O_VALID2)
            for e in range(4):
                full_group(x_halo, None, e, 4 + e, O_VALID, y1h, "y1h")
            for e in range(4):
                full_group(x_own, None, e, 8 + e, None, y1, "y1")
            L["l"] = 1
            prologue_layer(1)
            kv_pass(y1h, "y1h", O_VALID)
            for g in range(nown):
                full_group(y1, "y1", g, 4 + g, None, y, "y", g if debug else None)
        fw.op("sp", lambda e: None, [("y", g) for g in range(nown)]
              + ([("dbg", g) for g in range(nown)] if debug else []), [])

    fw.dry = True
    schedule()
    fw.dry = False
    wstate["next"] = 0
    bank["i"] = 0
    prologue()
    schedule()
    assert wstate["next"] == len(plan), (wstate["next"], len(plan))
    fw.emit()
    st.close()
    return nc


def _masks():
    a = np.arange(128)
    mA = np.zeros((128, 17, 128), np.float32)
    for dlt in range(17):
        dist = 128 * dlt + a[None, :] - a[:, None]
        w = ((dist >= 0) & (dist <= 128)).astype(np.float32)
        w += ((dist >= 0) & (dist <= 512) & (dist % 4 == 0)).astype(np.float32)
        w += ((dist >= 0) & (dist <= 2048) & (dist % 16 == 0)).astype(np.float32)
        mA[:, dlt, :] = w
    mC = np.zeros((128, 2, 128), np.float32)
    for dlt in range(2):
        dist = 128 * dlt + a[None, :] - a[:, None]
        mC[:, dlt, :] = ((dist >= 0) & (dist <= 127)).astype(np.float32)
    return mA.reshape(128, 17 * 128), mC.reshape(128, 2 * 128)


def _pack_cst(l, valid, valid2, norm1_g, norm2_g, a_q_g, a_k_g, conv_w, conv_b, conv_ln_g, conv_ln_b,
              c_q_g, c_k_g, c_sinks):
    c = np.zeros((128, NCST), np.float32)
    c[:, O_G1:O_G1 + 16] = norm1_g[l].reshape(16, 128).T
    c[:, O_G2:O_G2 + 16] = norm2_g[l].reshape(16, 128).T
    cw = conv_w[l].T.reshape(4, 128, 31)
    c[:, O_CW:O_CW + 124] = cw.transpose(1, 0, 2).reshape(128, 124)
    c[:, O_CB:O_CB + 4] = conv_b[l].reshape(4, 128).T
    c[:, O_LG:O_LG + 4] = conv_ln_g[l].reshape(4, 128).T
    c[:, O_LB:O_LB + 4] = conv_ln_b[l].reshape(4, 128).T
    c[:, O_VALID] = valid
    c[:, O_AQG:O_AQG + 64] = a_q_g[l][None, :]
    c[:, O_AKG:O_AKG + 64] = a_k_g[l][None, :]
    c[:, O_CQG:O_CQG + 64] = c_q_g[l][None, :]
    c[:, O_CKG:O_CKG + 64] = c_k_g[l][None, :]
    c[:, O_SINK:O_SINK + 16] = c_sinks[l][None, :]
    c[:, O_VALID2] = valid2
    return c


_PROG = {}
_RUN_KW = {}
_LAST = {}


def _get_prog(debug=False, cfg_over=None, nlayers=2):
    key = (debug, repr(cfg_over), nlayers)
    if key not in _PROG:
        _PROG[key] = build_program(debug=debug, cfg_over=cfg_over, nlayers=nlayers)
    return _PROG[key]


_VEC_KEYS = ("norm1_g", "norm2_g", "a_q_g", "a_k_g", "conv_w", "conv_b", "conv_ln_g", "conv_ln_b",
             "c_q_g", "c_k_g", "c_sinks")


def run_model(x, params, nlayers=2, debug=False, cfg_over=None, layer=0):
    cfg_over = dict(cfg_over or {})
    if nlayers == 1:
        cfg_over["layer"] = layer
    nc = _get_prog(debug, cfg_over, nlayers)
    mA, mC = _masks()
    ident = np.eye(128, dtype=np.float32)
    in_maps = []
    zeros = np.zeros((T_OWN, D), np.float32)
    for c in range(NCORES):
        b, q = divmod(c, 4)
        own = np.ascontiguousarray(x[b, q * T_OWN:(q + 1) * T_OWN])
        halo = np.ascontiguousarray(x[b, (q - 1) * T_OWN:q * T_OWN]) if q >= 1 else zeros
        h2 = np.ascontiguousarray(x[b, (q - 2) * T_OWN:(q - 1) * T_OWN]) if q >= 2 else zeros
        valid, valid2 = float(q >= 1), float(q >= 2)
        lsel = [layer, layer] if nlayers == 1 else [0, 1]
        cst = np.stack([_pack_cst(l, valid, valid2, *[params[k] for k in _VEC_KEYS]) for l in lsel])
        in_maps.append({
            "x_own": own, "x_halo": halo, "x_h2": h2,
            "w_in": params["w_in"], "w_out": params["w_out"],
            "w_gate": params["w_gate"], "w_up": params["w_up"], "w_down": params["w_down"],
            "cst": cst, "ident": ident, "maskA": mA, "maskC": mC,
        })
    res = run_bass_kernel_spmd(nc, in_maps, core_ids=list(range(NCORES)), **_RUN_KW)
    _LAST["res"] = res
    out = np.zeros_like(x)
    for c in range(NCORES):
        b, q = divmod(c, 4)
        out[b, q * T_OWN:(q + 1) * T_OWN] = res.results[c]["y"]
    if debug:
        return out, [res.results[c]["dbg"] for c in range(NCORES)]
    return out


def kernel(x, norm1_g, w_in, a_q_g, a_k_g, conv_w, conv_b, conv_ln_g, conv_ln_b,
           c_q_g, c_k_g, c_sinks, w_out, norm2_g, w_gate, w_up, w_down):
    params = dict(norm1_g=norm1_g, w_in=w_in, a_q_g=a_q_g, a_k_g=a_k_g, conv_w=conv_w, conv_b=conv_b,
                  conv_ln_g=conv_ln_g, conv_ln_b=conv_ln_b, c_q_g=c_q_g, c_k_g=c_k_g, c_sinks=c_sinks,
                  w_out=w_out, norm2_g=norm2_g, w_gate=w_gate, w_up=w_up, w_down=w_down)
    params = {k: np.ascontiguousarray(np.asarray(v, dtype=np.float32)) for k, v in params.items()}
    xx = np.asarray(x, dtype=np.float32)
    for l in range(2):
        xx = run_model(xx, params, nlayers=1, layer=l)
    return xx
```
